# Optimizing a Trainium2 kernel written in Bass

```python
import jax
import jax.numpy as jnp
from jax import lax
import numpy as np

D_MODEL = 1024
BATCH = 32
SEQ = 2048
DEPTH = 2
DEC_BATCH = 16
DEC_SEQ = 2048
PAST_LEN = 128

N_META = 16
W_A = 512
A_BLOCKS = 8
A_BLOCK = W_A // A_BLOCKS
CONV_W = 4
LRU_C = 8.0
W_B = 512
RWKV_HEADS = 8
RWKV_HD = W_B // RWKV_HEADS
LORA_W = 64
LORA_A = 64
LORA_G = 128
B_COLS = 3 * W_B + 2 * LORA_W + 2 * LORA_A + LORA_G
RWKV_GN_EPS = 64e-5
W_C = 512
HGRN_HEADS = 4
HGRN_HD = W_C // HGRN_HEADS
HGRN_CHUNK = 16
RET_HEADS = 4
RET_DK = 64
RET_DV = 128
RET_QK = RET_HEADS * RET_DK
W_D = RET_HEADS * RET_DV
RET_CHUNK = 128
ROPE_BASE = 10000.0
N_BRANCH = 4
W_BRANCH = 512
IN_SIZES = (W_A, W_A, B_COLS, W_C, 2 * W_C, W_C, W_C, RET_QK, RET_QK, W_D, W_D, N_BRANCH * D_MODEL)
N_IN = 2 * W_A + B_COLS + 5 * W_C + 2 * RET_QK + 2 * W_D + N_BRANCH * D_MODEL
N_EXPERTS = 16
D_FF_EXPERT = 2 * D_MODEL
EC_CAPACITY = 2
ALPHA = (2 * DEPTH) ** 0.25
BETA = (8 * DEPTH) ** -0.25
LN_EPS = 1e-5
NORM_EPS = 1e-6

kernel_name = 'hybrid_bidir_encoder_ec_moe'


def split_last(u, sizes):
    offs = [int(o) for o in np.cumsum(sizes)[:-1]]
    return jnp.split(u, offs, axis=-1)


def flip_t(t):
    return jnp.flip(t, axis=1)


def layer_norm(x, g, b):
    xf = x.astype(jnp.float32)
    mu = jnp.mean(xf, -1, keepdims=True)
    var = jnp.mean(jnp.square(xf - mu), -1, keepdims=True)
    y = (xf - mu) * lax.rsqrt(var + LN_EPS) * g.astype(jnp.float32) + b.astype(jnp.float32)
    return y.astype(x.dtype)


def head_group_norm(y, eps):
    yf = y.astype(jnp.float32)
    mu = jnp.mean(yf, -1, keepdims=True)
    var = jnp.mean(jnp.square(yf - mu), -1, keepdims=True)
    return (yf - mu) * lax.rsqrt(var + eps)


def head_rms_norm(y, eps):
    yf = y.astype(jnp.float32)
    return yf * lax.rsqrt(jnp.mean(jnp.square(yf), -1, keepdims=True) + eps)


def to_chunks(t, c):
    bn, l, h, x = t.shape
    return t.reshape(bn, l // c, c, h, x).transpose(1, 0, 3, 2, 4).astype(jnp.float32)


def from_chunks(o):
    nc, bn, h, c, x = o.shape
    return o.transpose(1, 0, 3, 2, 4).reshape(bn, nc * c, h, x)


def linear_recurrence(a, b):
    def combine(e1, e2):
        a1, b1 = e1
        a2, b2 = e2
        return a1 * a2, a2 * b1 + b2
    _, h = lax.associative_scan(combine, (a, b), axis=1)
    return h


def rglru_branch(xa, gate, conv_w, conv_b, wr, br, wi, bi, lam):
    bn, l, _ = xa.shape
    xp = jnp.pad(xa, ((0, 0), (CONV_W // 2, CONV_W - 1 - CONV_W // 2), (0, 0)))
    xc = conv_b + sum(xp[:, j:j + l] * conv_w[j] for j in range(CONV_W))
    xb = xc.reshape(bn, l, A_BLOCKS, A_BLOCK)
    h = jnp.zeros((bn, l, W_A), jnp.float32)
    for d in range(2):
        r = jax.nn.sigmoid(jnp.einsum('blgi,gij->blgj', xb, wr[d]).reshape(bn, l, W_A) + br[d])
        i = jax.nn.sigmoid(jnp.einsum('blgi,gij->blgj', xb, wi[d]).reshape(bn, l, W_A) + bi[d])
        log_a = -LRU_C * jax.nn.softplus(-lam[d].astype(jnp.float32)) * r.astype(jnp.float32)
        a = jnp.exp(log_a)
        b = jnp.sqrt(-jnp.expm1(2.0 * log_a)) * (i * xc).astype(jnp.float32)
        if d == 0:
            h = h + linear_recurrence(a, b)
        else:
            h = h + flip_t(linear_recurrence(flip_t(a), flip_t(b)))
    return (jax.nn.gelu(gate.astype(jnp.float32)) * h).astype(xa.dtype)


def rwkv7_scan(r, w, k, v, kk, a):
    bn, l, h, n = r.shape

    def step(s, inp):
        r_t, w_t, k_t, v_t, kk_t, a_t = inp
        sa = jnp.einsum('bhij,bhj->bhi', s, kk_t)
        s = (s * w_t[:, :, None, :] - sa[..., None] * (kk_t * a_t)[:, :, None, :]
             + v_t[..., None] * k_t[:, :, None, :])
        return s, jnp.einsum('bhij,bhj->bhi', s, r_t)

    xs = tuple(jnp.swapaxes(t, 0, 1) for t in (r, w, k, v, kk, a))
    s0 = jnp.zeros((bn, h, n, n), jnp.float32)
    _, y = lax.scan(step, s0, xs)
    return jnp.swapaxes(y, 0, 1)


def rwkv7_branch(cols, mu, w0, w2, a0, a2, g2, k_k, k_a, r_k, lnx_g, lnx_b):
    bn, l, _ = cols.shape
    f32 = jnp.float32
    xp = jnp.pad(cols, ((0, 0), (1, 1), (0, 0)))
    cols = cols + mu * (0.5 * (xp[:, :-2] + xp[:, 2:]) - cols)
    r, k, v, wdn, adn, gdn = split_last(cols, (W_B, W_B, W_B, 2 * LORA_W, 2 * LORA_A, LORA_G))

    def hs(t):
        return t.reshape(bn, l, RWKV_HEADS, RWKV_HD).astype(f32)

    kk = hs(k * k_k)
    kk = kk / jnp.maximum(jnp.sqrt(jnp.sum(jnp.square(kk), -1, keepdims=True)), 1e-12)
    wdn = wdn.reshape(bn, l, 2, LORA_W)
    adn = adn.reshape(bn, l, 2, LORA_A)
    rh, vh = hs(r), hs(v)
    rk = r_k.reshape(RWKV_HEADS, RWKV_HD).astype(f32)
    y = jnp.zeros((bn, l, RWKV_HEADS, RWKV_HD), f32)
    bonus = jnp.zeros_like(y)
    for d in range(2):
        w_log = -jax.nn.softplus(-(w0[d] + jnp.tanh(wdn[:, :, d]) @ w2[d]).astype(f32)) - 0.5
        decay = hs(jnp.exp(-jnp.exp(w_log)))
        a = jax.nn.sigmoid(a0[d] + adn[:, :, d] @ a2[d])
        kd = k * (1.0 + (a - 1.0) * k_a)
        ah, kh = hs(a), hs(kd)
        if d == 0:
            y = y + rwkv7_scan(rh, decay, kh, vh, kk, ah)
        else:
            y = y + flip_t(rwkv7_scan(flip_t(rh), flip_t(decay), flip_t(kh), flip_t(vh),
                                      flip_t(kk), flip_t(ah)))
        bonus = bonus + jnp.sum(rh * kh * rk, -1, keepdims=True) * vh
    g = jnp.einsum('blr,rw->blw', jax.nn.sigmoid(gdn), g2).astype(f32)
    yn = head_group_norm(y, RWKV_GN_EPS).reshape(bn, l, W_B) * lnx_g + lnx_b
    return ((yn + bonus.reshape(bn, l, W_B)) * g).astype(cols.dtype)


def gla_chunk_scan(q, k, v, logf):
    bn, l, h, dk = q.shape
    dv = v.shape[-1]
    c = HGRN_CHUNK
    mask = jnp.tril(jnp.ones((c, c), bool))[:, :, None]

    def step(s, inp):
        q_c, k_c, v_c, lf_c = inp
        bcum = jnp.cumsum(lf_c, axis=-2)
        diff = bcum[:, :, :, None, :] - bcum[:, :, None, :, :]
        dec = jnp.exp(jnp.where(mask, diff, -jnp.inf))
        att = jnp.einsum('bhid,bhijd,bhjd->bhij', q_c, dec, k_c)
        o = (jnp.einsum('bhij,bhje->bhie', att, v_c)
             + jnp.einsum('bhid,bhde->bhie', q_c * jnp.exp(bcum), s))
        b_last = bcum[:, :, -1:, :]
        s = (jnp.exp(b_last[:, :, 0, :])[..., None] * s
             + jnp.einsum('bhjd,bhje->bhde', k_c * jnp.exp(b_last - bcum), v_c))
        return s, o

    s0 = jnp.zeros((bn, h, dk, dv), jnp.float32)
    _, o = lax.scan(step, s0, (to_chunks(q, c), to_chunks(k, c), to_chunks(v, c), to_chunks(logf, c)))
    return from_chunks(o)


def hgrn2_branch(q, f2, i, g, lb, norm_g):
    bn, l, _ = q.shape
    f32 = jnp.float32

    def hs(t):
        return t.reshape(bn, l, HGRN_HEADS, HGRN_HD)

    qh = hs(jax.nn.silu(q.astype(f32)) * HGRN_HD ** -0.5)
    ih = hs(i.astype(f32))
    f2 = f2.astype(f32).reshape(bn, l, 2, W_C)
    log_lb, log_1m_lb = jnp.log(lb), jnp.log1p(-lb)
    o = jnp.zeros((bn, l, HGRN_HEADS, HGRN_HD), f32)
    for d in range(2):
        fr = f2[:, :, d]
        logf = hs(jnp.logaddexp(log_lb, log_1m_lb + jax.nn.log_sigmoid(fr)))
        kh = hs((1.0 - lb) * jax.nn.sigmoid(-fr))
        if d == 0:
            o = o + gla_chunk_scan(qh, kh, ih, logf)
        else:
            o = o + flip_t(gla_chunk_scan(flip_t(qh), flip_t(kh), flip_t(ih), flip_t(logf)))
    on = head_rms_norm(o, NORM_EPS).reshape(bn, l, W_C) * norm_g
    return (on * jax.nn.silu(g.astype(f32))).astype(q.dtype)


def rotary(x, pos):
    half = x.shape[-1] // 2
    inv = ROPE_BASE ** (-jnp.arange(half, dtype=jnp.float32) / half)
    ang = pos.astype(jnp.float32)[:, None] * inv
    cos, sin = jnp.cos(ang)[:, None, :], jnp.sin(ang)[:, None, :]
    xf = x.astype(jnp.float32)
    x1, x2 = xf[..., :half], xf[..., half:]
    return jnp.concatenate([x1 * cos - x2 * sin, x1 * sin + x2 * cos], -1).astype(x.dtype)


def retention_chunk_scan(q, k, v, log_gamma):
    c = RET_CHUNK
    bn, lp, h, dk = q.shape
    dv = v.shape[-1]
    idx = jnp.arange(c, dtype=jnp.float32)
    rel = idx[:, None] - idx[None, :]
    dmat = jnp.where(rel >= 0, jnp.exp(jnp.maximum(rel, 0.0) * log_gamma[:, None, None]), 0.0)
    xi = jnp.exp((idx + 1.0) * log_gamma[:, None])[..., None]
    zeta = jnp.exp((c - 1.0 - idx) * log_gamma[:, None])[..., None]
    g_c = jnp.exp(c * log_gamma)[:, None, None]

    def step(state, inp):
        q_c, k_c, v_c = inp
        att = jnp.einsum('bhid,bhjd->bhij', q_c, k_c) * dmat
        o = (jnp.einsum('bhij,bhje->bhie', att, v_c)
             + jnp.einsum('bhid,bhde->bhie', q_c, state) * xi)
        state = g_c * state + jnp.einsum('bhjd,bhje->bhde', k_c * zeta, v_c)
        return state, o

    s0 = jnp.zeros((bn, h, dk, dv), jnp.float32)
    _, o = lax.scan(step, s0, (to_chunks(q, c), to_chunks(k, c), to_chunks(v, c)))
    return from_chunks(o)


def retention_branch(q, k, v, g, decay_logit):
    bn, l, _ = q.shape
    pos = jnp.arange(l)
    qh = rotary(q.reshape(bn, l, RET_HEADS, RET_DK), pos)
    kh = rotary(k.reshape(bn, l, RET_HEADS, RET_DK), pos) * RET_DK ** -0.5
    vh = v.reshape(bn, l, RET_HEADS, RET_DV)
    n_pad = RET_CHUNK - N_META
    pad = ((0, 0), (n_pad, 0), (0, 0), (0, 0))
    qh, kh, vh = jnp.pad(qh, pad), jnp.pad(kh, pad), jnp.pad(vh, pad)
    log_gamma = jax.nn.log_sigmoid(decay_logit.astype(jnp.float32))
    o = (retention_chunk_scan(qh, kh, vh, log_gamma[0])
         + flip_t(retention_chunk_scan(flip_t(qh), flip_t(kh), flip_t(vh), log_gamma[1])))[:, n_pad:]
    on = head_group_norm(o, NORM_EPS).reshape(bn, l, W_D)
    return (on * jax.nn.silu(g.astype(jnp.float32))).astype(q.dtype)


def token_mixer(h, p, li, lb):
    bn, l, _ = h.shape
    u = jnp.einsum('bld,dn->bln', h, p['w_in'][li])
    a_x, a_g, b_cols, c_q, c_f, c_i, c_g, d_q, d_k, d_v, d_g, m = split_last(u, IN_SIZES)
    o_a = rglru_branch(a_x, a_g, p['conv_w'][li], p['conv_b'][li], p['lru_wr'][li], p['lru_br'][li],
                       p['lru_wi'][li], p['lru_bi'][li], p['lru_lambda'][li])
    o_b = rwkv7_branch(b_cols, p['rwkv_mu'][li], p['rwkv_w0'][li], p['rwkv_w2'][li], p['rwkv_a0'][li],
                       p['rwkv_a2'][li], p['rwkv_g2'][li], p['rwkv_kk'][li], p['rwkv_ka'][li],
                       p['rwkv_rk'][li], p['rwkv_lnx_g'][li], p['rwkv_lnx_b'][li])
    o_c = hgrn2_branch(c_q, c_f, c_i, c_g, lb, p['hgrn_norm_g'][li])
    o_d = retention_branch(d_q, d_k, d_v, d_g, p['ret_decay'][li])
    branches = jnp.stack([o_a, o_b, o_c, o_d], axis=2)
    proj = jnp.einsum('blnw,nwd->blnd', branches, p['w_branch'][li])
    gates = jax.nn.sigmoid(m.reshape(bn, l, N_BRANCH, D_MODEL))
    merged = jnp.sum(gates * proj, axis=2)
    return jnp.einsum('bld,de->ble', merged, p['w_out'][li])


def expert_choice_ffn(x, router, w1, w3, w2):
    n = x.shape[0]
    cap = EC_CAPACITY * n // N_EXPERTS
    aff = jax.nn.softmax(jnp.einsum('nd,de->ne', x, router).astype(jnp.float32), axis=-1)
    gate, idx = lax.top_k(aff.T, cap)

    def one_expert(args):
        w1e, w3e, w2e, ie, ge = args
        xe = x[ie]
        he = jax.nn.silu(xe @ w1e) * (xe @ w3e)
        return (he @ w2e) * ge[:, None].astype(x.dtype)

    ye = lax.map(one_expert, (w1, w3, w2, idx, gate))
    return jnp.zeros_like(x).at[idx.reshape(-1)].add(ye.reshape(-1, x.shape[-1]))


def encode(x, p, lb_all):
    bn = x.shape[0]
    meta = jnp.broadcast_to(p['meta'][None].astype(x.dtype), (bn, N_META, D_MODEL))
    h = layer_norm(jnp.concatenate([meta, x], axis=1), p['ln_emb_g'], p['ln_emb_b'])
    for li in range(DEPTH):
        h = layer_norm(ALPHA * h + token_mixer(h, p, li, lb_all[li]), p['ln1_g'][li], p['ln1_b'][li])
        bn, l, _ = h.shape
        moe = expert_choice_ffn(h.reshape(bn * l, D_MODEL), p['router'][li], p['exp_w1'][li],
                                p['exp_w3'][li], p['exp_w2'][li]).reshape(bn, l, D_MODEL)
        h = layer_norm(ALPHA * h + moe, p['ln2_g'][li], p['ln2_b'][li])
    return h[:, N_META:]


def setup_inputs(seed: int = 0) -> dict:
    key = jax.random.key(seed)
    ks = iter(jax.random.split(key, 48))
    f32 = jnp.float32

    def nrm(shape, scale):
        return jax.random.normal(next(ks), shape, f32) * scale

    def unif(shape, lo, hi):
        return jax.random.uniform(next(ks), shape, f32, lo, hi)

    x_prompt = nrm((BATCH, SEQ, D_MODEL), 1.0)
    x_sample = nrm((DEC_BATCH, DEC_SEQ, D_MODEL), 1.0)
    meta = nrm((N_META, D_MODEL), 1.0)
    ln_emb_g = 1.0 + nrm((D_MODEL,), 0.02)
    ln_emb_b = nrm((D_MODEL,), 0.02)
    hgrn_lb = nrm((DEPTH, W_C), 0.5)
    w_in = nrm((DEPTH, D_MODEL, N_IN), D_MODEL ** -0.5)
    conv_w = nrm((DEPTH, CONV_W, W_A), CONV_W ** -0.5)
    conv_b = nrm((DEPTH, W_A), 0.02)
    lru_wr = nrm((DEPTH, 2, A_BLOCKS, A_BLOCK, A_BLOCK), A_BLOCK ** -0.5)
    lru_br = nrm((DEPTH, 2, W_A), 0.02)
    lru_wi = nrm((DEPTH, 2, A_BLOCKS, A_BLOCK, A_BLOCK), A_BLOCK ** -0.5)
    lru_bi = nrm((DEPTH, 2, W_A), 0.02)
    u = unif((DEPTH, 2, W_A), 0.9, 0.999)
    a_base = u ** (1.0 / LRU_C)
    lru_lambda = jnp.log(a_base) - jnp.log1p(-a_base)
    rwkv_mu = unif((DEPTH, B_COLS), 0.0, 1.0)
    rwkv_w0 = unif((DEPTH, 2, W_B), -6.0, 1.0)
    rwkv_w2 = nrm((DEPTH, 2, LORA_W, W_B), 0.5 * LORA_W ** -0.5)
    rwkv_a0 = nrm((DEPTH, 2, W_B), 0.1)
    rwkv_a2 = nrm((DEPTH, 2, LORA_A, W_B), 0.5 * LORA_A ** -0.5)
    rwkv_g2 = nrm((DEPTH, LORA_G, W_B), LORA_G ** -0.5)
    rwkv_kk = 0.85 + nrm((DEPTH, W_B), 0.02)
    rwkv_ka = 1.0 + nrm((DEPTH, W_B), 0.02)
    rwkv_rk = nrm((DEPTH, W_B), 0.1)
    rwkv_lnx_g = 1.0 + nrm((DEPTH, W_B), 0.02)
    rwkv_lnx_b = nrm((DEPTH, W_B), 0.02)
    hgrn_norm_g = 1.0 + nrm((DEPTH, W_C), 0.02)
    gamma0 = 1.0 - jnp.exp2(-5.0 - jnp.arange(RET_HEADS, dtype=f32))
    ret_decay = (jnp.log(gamma0) - jnp.log1p(-gamma0)) + nrm((DEPTH, 2, RET_HEADS), 0.01)
    w_branch = nrm((DEPTH, N_BRANCH, W_BRANCH, D_MODEL), W_BRANCH ** -0.5)
    w_out = nrm((DEPTH, D_MODEL, D_MODEL), BETA * D_MODEL ** -0.5)
    ln1_g = 1.0 + nrm((DEPTH, D_MODEL), 0.02)
    ln1_b = nrm((DEPTH, D_MODEL), 0.02)
    router = nrm((DEPTH, D_MODEL, N_EXPERTS), D_MODEL ** -0.5)
    exp_w1 = nrm((DEPTH, N_EXPERTS, D_MODEL, D_FF_EXPERT), D_MODEL ** -0.5)
    exp_w3 = nrm((DEPTH, N_EXPERTS, D_MODEL, D_FF_EXPERT), D_MODEL ** -0.5)
    exp_w2 = nrm((DEPTH, N_EXPERTS, D_FF_EXPERT, D_MODEL), BETA * D_FF_EXPERT ** -0.5)
    ln2_g = 1.0 + nrm((DEPTH, D_MODEL), 0.02)
    ln2_b = nrm((DEPTH, D_MODEL), 0.02)
    return {'x_prompt': x_prompt, 'x_sample': x_sample, 'meta': meta, 'ln_emb_g': ln_emb_g,
            'ln_emb_b': ln_emb_b, 'hgrn_lb': hgrn_lb, 'w_in': w_in, 'conv_w': conv_w, 'conv_b': conv_b,
            'lru_wr': lru_wr, 'lru_br': lru_br, 'lru_wi': lru_wi, 'lru_bi': lru_bi,
            'lru_lambda': lru_lambda, 'rwkv_mu': rwkv_mu, 'rwkv_w0': rwkv_w0, 'rwkv_w2': rwkv_w2,
            'rwkv_a0': rwkv_a0, 'rwkv_a2': rwkv_a2, 'rwkv_g2': rwkv_g2, 'rwkv_kk': rwkv_kk,
            'rwkv_ka': rwkv_ka, 'rwkv_rk': rwkv_rk, 'rwkv_lnx_g': rwkv_lnx_g, 'rwkv_lnx_b': rwkv_lnx_b,
            'hgrn_norm_g': hgrn_norm_g, 'ret_decay': ret_decay, 'w_branch': w_branch, 'w_out': w_out,
            'ln1_g': ln1_g, 'ln1_b': ln1_b, 'router': router, 'exp_w1': exp_w1, 'exp_w3': exp_w3,
            'exp_w2': exp_w2, 'ln2_g': ln2_g, 'ln2_b': ln2_b}


def reference(x_prompt, x_sample, meta, ln_emb_g, ln_emb_b, hgrn_lb, w_in, conv_w, conv_b,
              lru_wr, lru_br, lru_wi, lru_bi, lru_lambda, rwkv_mu, rwkv_w0, rwkv_w2, rwkv_a0,
              rwkv_a2, rwkv_g2, rwkv_kk, rwkv_ka, rwkv_rk, rwkv_lnx_g, rwkv_lnx_b, hgrn_norm_g,
              ret_decay, w_branch, w_out, ln1_g, ln1_b, router, exp_w1, exp_w3, exp_w2, ln2_g, ln2_b):
    p = dict(meta=meta, ln_emb_g=ln_emb_g, ln_emb_b=ln_emb_b, w_in=w_in, conv_w=conv_w,
             conv_b=conv_b, lru_wr=lru_wr, lru_br=lru_br, lru_wi=lru_wi, lru_bi=lru_bi,
             lru_lambda=lru_lambda, rwkv_mu=rwkv_mu, rwkv_w0=rwkv_w0, rwkv_w2=rwkv_w2,
             rwkv_a0=rwkv_a0, rwkv_a2=rwkv_a2, rwkv_g2=rwkv_g2, rwkv_kk=rwkv_kk, rwkv_ka=rwkv_ka,
             rwkv_rk=rwkv_rk, rwkv_lnx_g=rwkv_lnx_g, rwkv_lnx_b=rwkv_lnx_b, hgrn_norm_g=hgrn_norm_g,
             ret_decay=ret_decay, w_branch=w_branch, w_out=w_out, ln1_g=ln1_g, ln1_b=ln1_b,
             router=router, exp_w1=exp_w1, exp_w3=exp_w3, exp_w2=exp_w2, ln2_g=ln2_g, ln2_b=ln2_b)
    cum = jnp.cumsum(jax.nn.softmax(hgrn_lb.astype(jnp.float32), axis=0), axis=0)
    lb_all = cum - cum[:1]
    y_prompt = encode(x_prompt, p, lb_all)
    y_sample = encode(x_sample, p, lb_all)
    return (y_prompt, y_sample)
```

```python
import contextlib
import numpy as np
import concourse.bass as bass
import concourse.mybir as mybir
from concourse.bass_utils import run_bass_kernel_spmd

F32 = mybir.dt.float32
BF16 = mybir.dt.bfloat16
I32 = mybir.dt.int32
AF = mybir.ActivationFunctionType
ALU = mybir.AluOpType
AX = mybir.AxisListType

D = 1024
L = 2064
NMETA = 16
NT = 17
NIN = 11136
ALPHA = 4.0 ** 0.25
BIGF = 1.0e30
COL = dict(ax=0, ag=512, br=1024, bk=1536, bv=2048, bw=2560, ba=2688, bg=2816, cq=2944, cf=3456, ci=4480,
           cg=4992, dq=5504, dk=5760, dv=6016, dg=6528, m=7040)
TOKB = [(0, 16)] + [(16 + 512 * i, 512) for i in range(4)]
NPP = 64
PP = dict(cw=0, cb=16, br=20, bi=28, lam=36, lb0=44, lb1=48, ng=52, rdec=56)
CAPL = 2048
NEXP = 16


def trows(j):
    return 16 if j == 0 else 128


def toff(j):
    return 0 if j == 0 else 16 + 128 * (j - 1)


class Sched:
    ENG = ('pe', 'dve', 'act', 'pool', 'sp')

    def __init__(self, nc, ndma=24):
        self.nc = nc
        self.E = {'pe': nc.tensor, 'dve': nc.vector, 'act': nc.scalar, 'pool': nc.gpsimd, 'sp': nc.sync}
        self.sem = {k: nc.alloc_semaphore('sc_' + k) for k in self.ENG}
        self.ndma = ndma
        for i in range(ndma):
            self.sem[('d', i)] = nc.alloc_semaphore(f'sd_{i}')
        self.nbank = 0
        self._reset()

    def _reset(self):
        self.cnt = {k: 0 for k in self.sem}
        self.known = {e: {} for e in self.ENG}
        self.lastw = {}
        self.readers = {}
        self.dnext = 0

    def _wait(self, eng, tok):
        key, val = tok
        if self.known[eng].get(key, 0) >= val:
            return
        self.E[eng].wait_ge(self.sem[key], val)
        self.known[eng][key] = val

    def _deps(self, eng, r, w, same_ok=False):
        deps = []
        for x in r:
            t = self.lastw.get(x)
            if t is not None:
                deps.append(t)
        for x in w:
            t = self.lastw.get(x)
            if t is not None:
                deps.append(t)
            deps.extend(self.readers.get(x, ()))
        for t in deps:
            if same_ok and t[0] == eng:
                continue
            self._wait(eng, t)

    def _record(self, tok, r, w):
        for x in r:
            self.readers.setdefault(x, []).append(tok)
        for x in w:
            self.lastw[x] = tok
            self.readers[x] = []

    def op(self, eng, fn, r=(), w=()):
        self._deps(eng, r, w, same_ok=(eng == 'pe'))
        ins = fn(self.E[eng])
        self.cnt[eng] += 1
        ins.then_inc(self.sem[eng], 1)
        tok = (eng, self.cnt[eng])
        self._record(tok, r, w)
        return tok

    def dma(self, fn, r=(), w=(), q='sp'):
        slot = self.dnext
        self.dnext = (slot + 1) % self.ndma
        key = ('d', slot)
        if self.cnt[key] > 0:
            self._wait(q, (key, self.cnt[key]))
        self._deps(q, r, w)
        ins = fn(self.E[q])
        self.cnt[key] += 16
        ins.then_inc(self.sem[key], 16)
        tok = (key, self.cnt[key])
        self._record(tok, r, w)
        return tok

    def join(self, engines=None):
        for e in (engines or self.ENG):
            for key, c in self.cnt.items():
                if c > 0:
                    self._wait(e, (key, c))

    def sync_clear(self):
        self.join()
        self.nc.all_engine_barrier()
        for key in self.sem:
            self.nc.gpsimd.sem_clear(self.sem[key])
        self.nc.all_engine_barrier()
        self._reset()

    @contextlib.contextmanager
    def loop(self, n):
        self.sync_clear()
        if n == 1:
            yield 0
            self.sync_clear()
            return
        with self.nc.Fori(0, n) as i:
            yield i
            self.sync_clear()

    def iter(self, n, unroll=False):
        if unroll or n == 1:
            for s in range(n):
                yield s
        else:
            with self.loop(n) as s:
                yield s

    def bank(self):
        b = self.nbank
        self.nbank = (b + 1) % 8
        return b


class Builder:
    def __init__(self, nseq, nsp, ncore, launch, dbg=None):
        self.nseq, self.nsp, self.ncore, self.launch = nseq, nsp, ncore, launch
        self.dbg = dbg or set()
        self.nc = bass.Bass("TRN2", target_bir_lowering=False)
        self.es = contextlib.ExitStack()
        self.S = Sched(self.nc)
        self.tiles = {}
        self.ins = {}
        self.outs = {}
        self.ps = [self.es.enter_context(self.nc.psum_tensor(f"ps{i}", [128, 512], F32)) for i in range(8)]
        self.bcreg = self.nc.gpsimd.alloc_register("bcreg")
        self.nc.gpsimd.reg_mov(self.bcreg, CAPL - 1)
        self.bcreg2 = self.nc.gpsimd.alloc_register("bcreg2")
        self.nc.gpsimd.reg_mov(self.bcreg2, CAPL + 127)
        self.Hcur = self.dscr('Hcur', [L, D])
        self.Xcur = self.dscr('Xcur', [2048, D])
        self.OTc = self.dscr('OTc', [512, L], BF16)
        self.SCc = self.dscr('SCc', [2, L, 6, 512])
        self.BONc = self.dscr('BONc', [L, 512])
        self.GATc = self.dscr('GATc', [L, 512])
        self.Yc = self.dscr('Yc', [2, L, 512])
        self.OT4c = self.dscr('OT4c', [4, 512, L], BF16)
        self.AFFc = self.dscr('AFFc', [L, 16])
        self.XBc = self.dscr('XBc', [CAPL, D], BF16)
        self.YBc = self.dscr('YBc', [CAPL, D])
        self.XST = self.dscr('XST', [2, nseq, 16, 6, 512])
        self.YST = self.dscr('YST', [2, nseq, 16, 512])

    def din(self, name, shape, dt=F32):
        t = self.nc.dram_tensor(name, list(shape), dt, kind="ExternalInput").ap()
        self.ins[name] = t
        return t

    def dout(self, name, shape, dt=F32):
        t = self.nc.dram_tensor(name, list(shape), dt, kind="ExternalOutput").ap()
        self.outs[name] = t
        return t

    def dscr(self, name, shape, dt=F32):
        return self.nc.dram_tensor(name, list(shape), dt).ap()

    def tile(self, name, shape, dt=F32):
        self.uid = getattr(self, 'uid', 0) + 1
        t = self.es.enter_context(self.nc.sbuf_tensor(f"{name}_u{self.uid}", list(shape), dt))
        return t

    @contextlib.contextmanager
    def scope(self):
        es = contextlib.ExitStack()
        old = self.es
        self.es = es
        try:
            yield
        finally:
            self.S.sync_clear()
            self.es = old
            es.close()

    def tt(self, out, a, b, op, r, w, eng='dve'):
        return self.S.op(eng, lambda e: e.tensor_tensor(out=out, in0=a, in1=b, op=op), r=r, w=w)

    def ts(self, out, a, s1, s2, op0, op1, r, w, eng='dve'):
        if op1 is None:
            return self.S.op(eng, lambda e: e.tensor_scalar(out=out, in0=a, scalar1=s1, scalar2=None, op0=op0), r=r, w=w)
        return self.S.op(eng, lambda e: e.tensor_scalar(out=out, in0=a, scalar1=s1, scalar2=s2, op0=op0, op1=op1), r=r, w=w)

    def stt(self, out, a, s, b, op0, op1, r, w):
        return self.S.op('dve', lambda e: e.scalar_tensor_tensor(out=out, in0=a, scalar=s, in1=b, op0=op0, op1=op1), r=r, w=w)

    def act(self, out, in_, func, r, w, bias=None, scale=None):
        kw = {}
        if bias is not None:
            kw['bias'] = bias
        if scale is not None:
            kw['scale'] = scale
        return self.S.op('act', lambda e: e.activation(out=out, in_=in_, func=func, **kw), r=r, w=w)

    def cp(self, out, in_, r, w, eng='dve'):
        if eng == 'act':
            return self.act(out, in_, AF.Copy, r, w)
        return self.S.op(eng, lambda e: e.tensor_copy(out=out, in_=in_), r=r, w=w)

    def ms(self, ap, val, w, eng='pool'):
        return self.S.op(eng, lambda e: e.memset(ap, val), w=w)

    def ld(self, out, in_, r, w, q='sp'):
        return self.S.dma(lambda e: e.dma_start(out=out, in_=in_), r=r, w=w, q=q)

    def mm(self, out, pairs, r, w):
        n = len(pairs)

        def fn(e):
            ins = None
            for i, (a, b) in enumerate(pairs):
                ins = e.matmul(out, lhsT=a, rhs=b, start=(i == 0), stop=(i == n - 1))
            return ins
        return self.S.op('pe', fn, r=r, w=w)

    def mms(self, groups, r, w):
        def fn(e):
            ins = None
            for out, pairs in groups:
                n = len(pairs)
                for i, (a, b) in enumerate(pairs):
                    ins = e.matmul(out, lhsT=a, rhs=b, start=(i == 0), stop=(i == n - 1))
            return ins
        return self.S.op('pe', fn, r=r, w=w)

    def tr(self, out, in_, ident, r, w):
        return self.S.op('pe', lambda e: e.transpose(out=out, in_=in_, identity=ident), r=r, w=w)

    def load_consts(self):
        c = self.c = {}
        c['ident'] = self.tile("ident", [128, 128])
        self.ld(c['ident'][:], self.ins['c_ident'], [], ['ident'])
        c['identb'] = self.tile("identb", [128, 128], BF16)
        self.cp(c['identb'][:], c['ident'][:], ['ident'], ['identb'])
        c['eps_ln'] = self.tile("eps_ln", [128, 1])
        self.ms(c['eps_ln'][:], 1e-5, ['eps_ln'])

    def build_hT(self, H, s, hT, ident):
        S = self.S
        self.ld(self.Hcur, H[s], ['H'], ['Hcur'], q=self.dq)
        for j in range(NT):
            rows, t0 = trows(j), toff(j)
            hx = self.tl_hx[j % 2]
            self.ld(hx[0:rows, :], self.Hcur[t0:t0 + rows, :], ['Hcur'], [f'hx{j % 2}'])
            for g in range(2):
                b = S.bank()
                def fn(e, g=g, b=b, rows=rows, hx=hx):
                    ins = None
                    for q in range(4):
                        kc = g * 4 + q
                        ins = e.transpose(out=self.ps[b][:, q * 128:q * 128 + rows], in_=hx[0:rows, kc * 128:(kc + 1) * 128],
                                          identity=ident[0:rows, 0:rows])
                    return ins
                S.op('pe', fn, r=[f'hx{j % 2}', 'ident'], w=[f'ps{b}'])
                src = self.ps[b][:, :].rearrange("p (q c) -> p q c", q=4)[:, :, 0:rows]
                dst = hT[:, g * 4:(g + 1) * 4, t0:t0 + rows]
                self.cp(dst, src, [f'ps{b}'], ['hT'], eng=('act' if g == 0 else 'dve'))

    def layer_norm_tile(self, x, rows, g_b, b_b, out, rkey, wkey):
        st, mv = self.t_st, self.t_mv
        S = self.S
        for q in range(2):
            S.op('dve', lambda e, q=q: e.bn_stats(out=st[0:rows, q, :], in_=x[:, q * 512:(q + 1) * 512]), r=[rkey], w=['ln_st'])
        S.op('dve', lambda e: e.bn_aggr(out=mv[0:rows, 0:2], in_=st[0:rows, :, :]), r=['ln_st'], w=['ln_mv'])
        self.act(mv[0:rows, 2:3], mv[0:rows, 1:2], AF.Sqrt, ['ln_mv', 'eps_ln'], ['ln_mv2'], bias=self.c['eps_ln'][0:rows, :])
        S.op('dve', lambda e: e.reciprocal(out=mv[0:rows, 3:4], in_=mv[0:rows, 2:3]), r=['ln_mv2'], w=['ln_mv3'])
        self.ts(x, x, mv[0:rows, 0:1], mv[0:rows, 3:4], ALU.subtract, ALU.mult, [rkey, 'ln_mv', 'ln_mv3'], [rkey])
        self.tt(x, x, g_b[0:rows, :], ALU.mult, [rkey, 'lng'], [rkey], eng='pool')
        self.tt(out, x, b_b[0:rows, :], ALU.add, [rkey, 'lnb'], [wkey])

    def alloc_ln(self):
        self.t_st = self.tile("ln_st", [128, 2, 6])
        self.t_mv = self.tile("ln_mv", [128, 4])

    def load_wblk(self, wb, key, WIN, li, c0, ncols):
        self.ld(wb[:, :, 0:ncols], WIN[li, :, c0:c0 + ncols].rearrange("(k p) n -> p k n", p=128), ['WIN'], [key])

    def proj_fm(self, wb, wkey, c_lo, ncols, hT, evac, nk=8, hkeys=('hT',)):
        for (t0, ntok) in TOKB:
            b = self.S.bank()
            pairs = [(wb[:, k, c_lo:c_lo + ncols], hT[:, k, t0:t0 + ntok]) for k in range(nk)]
            self.mm(self.ps[b][0:ncols, 0:ntok], pairs, [wkey] + list(hkeys), [f'ps{b}'])
            evac(self.ps[b][0:ncols, 0:ntok], t0, ntok, f'ps{b}')

    def phase_W(self, li, WIN, WMIX, WBR, WOUT, WSW):
        S = self.S
        w_in, w_branch, w_out, mu = self.ins['w_in'], self.ins['w_branch'], self.ins['w_out'], self.ins['rwkv_mu']
        with self.scope():
            st = [self.tile(f"wst{i}", [128, 3712]) for i in range(2)]
            sb = [self.tile(f"wsb{i}", [128, 3712], BF16) for i in range(2)]
            mub = self.tile("mub", [128, 1920])
            omm = self.tile("omm", [128, 1920])
            self.ld(mub[:], mu[li].partition_broadcast(128), [], ['mub'])
            self.ts(omm[:], mub[:], -1.0, 1.0, ALU.mult, ALU.add, ['mub'], ['omm'])
            self.ts(mub[:], mub[:], 0.5, None, ALU.mult, None, ['mub', 'omm'], ['mub'])
            engs = ['act', 'dve', 'pool']
            n = 0

            def cast(src, dst, ncols):
                nonlocal n
                i = n % 2
                self.ld(st[i][:, 0:ncols], src, [], [f'wst{i}'])
                self.cp(sb[i][:, 0:ncols], st[i][:, 0:ncols], [f'wst{i}'], [f'wsb{i}'], eng=engs[n % 3])
                self.ld(dst, sb[i][:, 0:ncols], [f'wsb{i}'], ['WDRAM'])
                n += 1
            for kc in range(8):
                for q in range(3):
                    cast(w_in[li, kc * 128:(kc + 1) * 128, q * 3712:(q + 1) * 3712],
                         WIN[li, kc * 128:(kc + 1) * 128, q * 3712:(q + 1) * 3712], 3712)
            for kc in range(8):
                i = n % 2
                self.ld(st[i][:, 0:1920], w_in[li, kc * 128:(kc + 1) * 128, 1024:2944], [], [f'wst{i}'])
                self.tt(sb[i][:, 0:1920], st[i][:, 0:1920], omm[:], ALU.mult, [f'wst{i}', 'omm'], [f'wsb{i}'])
                self.ld(WMIX[li, kc * 128:(kc + 1) * 128, :], sb[i][:, 0:1920], [f'wsb{i}'], ['WDRAM'])
                n += 1
                j = n % 2
                self.tt(sb[j][:, 0:1920], st[i][:, 0:1920], mub[:], ALU.mult, [f'wst{i}', 'mub'], [f'wsb{j}'], eng='pool')
                self.ld(WMIX[li, 1024 + kc * 128:1024 + (kc + 1) * 128, :], sb[j][:, 0:1920], [f'wsb{j}'], ['WDRAM'])
                n += 1
            wbr = w_branch[li].rearrange("n w d -> (n w) d")
            for kc in range(16):
                cast(wbr[kc * 128:(kc + 1) * 128, :], WBR[li, kc * 128:(kc + 1) * 128, :], 1024)
            for kc in range(8):
                cast(w_out[li, kc * 128:(kc + 1) * 128, :], WOUT[li, kc * 128:(kc + 1) * 128, :], 1024)
            for kc in range(8):
                cast(self.ins['w_sw'][li, kc * 128:(kc + 1) * 128, :], WSW[li, kc * 128:(kc + 1) * 128, :], 512)

    def phase_E(self, H):
        S = self.S
        x, meta = self.ins['x'], self.ins['meta']
        with self.scope():
            self.load_consts()
            self.alloc_ln()
            g_b = self.tile("lng", [128, D])
            b_b = self.tile("lnb", [128, D])
            self.ld(g_b[:], self.ins['ln_emb_g'].partition_broadcast(128), [], ['lng'])
            self.ld(b_b[:], self.ins['ln_emb_b'].partition_broadcast(128), [], ['lnb'])
            xt = [self.tile(f"ex{i}", [128, D]) for i in range(2)]
            ot = [self.tile(f"eo{i}", [128, D]) for i in range(2)]
            self.dq = 'sp'
            for s in S.iter(self.nseq, True):
                self.ld(self.Xcur, x[s], [], ['Xcur'])
                for j in range(NT):
                    rows, t0 = trows(j), toff(j)
                    i = j % 2
                    if j == 0:
                        self.ld(xt[i][0:rows, :], meta, [], [f'ex{i}'])
                    else:
                        self.ld(xt[i][0:rows, :], self.Xcur[(j - 1) * 128:j * 128, :], ['Xcur'], [f'ex{i}'])
                    self.layer_norm_tile(xt[i][0:rows, :], rows, g_b, b_b, ot[i][0:rows, :], f'ex{i}', f'eo{i}')
                    self.ld(self.Hcur[t0:t0 + rows, :], ot[i][0:rows, :], [f'eo{i}'], ['Hcur'])
                self.ld(H[s], self.Hcur, ['Hcur'], ['H'])

    def phase_A(self, li, H, WIN, OT):
        S = self.S
        with self.scope():
            self.load_consts()
            hT = self.tile("hT", [128, 8, L], BF16)
            self.tl_hx = [self.tile(f"hx{i}", [128, D]) for i in range(2)]
            wb = [self.tile(f"wbA{i}", [128, 8, 512], BF16) for i in range(2)]
            F = [self.tile(f"F{i}", [128, L]) for i in range(7)]
            G = [self.tile(f"G{i}", [128, L], BF16) for i in range(2)]
            ppt = self.tile("ppt", [128, NPP])
            self.ld(ppt[:], self.ins['pp'][li], [], ['ppt'])
            one = self.tile("one", [128, 1])
            self.ms(one[:], 1.0, ['one'])
            bd32 = self.tile("bd32", [128, 16, 128])
            bd = self.tile("bd", [128, 16, 128], BF16)
            self.ld(bd32[:], self.ins['lru_bd'][li].rearrange("g p q -> p g q"), [], ['bd32'])
            self.cp(bd[:], bd32[:], ['bd32'], ['bd'])
            sc8 = self.tile("sc8", [128, 8])
            lam = ppt[:, PP['lam']:PP['lam'] + 8]
            self.act(sc8[:], lam, AF.Exp, ['ppt'], ['sc8'], scale=-1.0)
            self.ts(sc8[:], sc8[:], 1.0, None, ALU.add, None, ['sc8'], ['sc8'])
            self.act(sc8[:], sc8[:], AF.Ln, ['sc8'], ['sc8'])
            self.ts(sc8[:], sc8[:], -8.0, None, ALU.mult, None, ['sc8'], ['sc8'])
            self.load_wblk(wb[0], 'wbA0', WIN, li, COL['ax'], 512)
            self.load_wblk(wb[1], 'wbA1', WIN, li, COL['ag'], 512)
            xa, gate, xc, hsum, Fr, Fi, Ft = F
            self.dq = 'sp'
            for s in S.iter(self.nseq, True):
                self.build_hT(H, s, hT, self.c['ident'])
                for ct in range(4):
                    def ev_x(ps, t0, n, bk):
                        self.act(xa[:, t0:t0 + n], ps, AF.Copy, [bk], ['F0'])
                    def ev_g(ps, t0, n, bk):
                        self.act(gate[:, t0:t0 + n], ps, AF.Gelu_apprx_tanh, [bk], ['F1'])
                    self.proj_fm(wb[0], 'wbA0', ct * 128, 128, hT, ev_x)
                    self.proj_fm(wb[1], 'wbA1', ct * 128, 128, hT, ev_g)
                    cw = lambda j: ppt[:, PP['cw'] + ct * 4 + j:PP['cw'] + ct * 4 + j + 1]
                    cb = ppt[:, PP['cb'] + ct:PP['cb'] + ct + 1]
                    self.ts(xc[:, :], xa[:, :], cw(2), cb, ALU.mult, ALU.add, ['F0', 'ppt'], ['F2'])
                    self.stt(xc[:, 2:L], xa[:, 0:L - 2], cw(0), xc[:, 2:L], ALU.mult, ALU.add, ['F0', 'F2'], ['F2'])
                    self.stt(xc[:, 1:L], xa[:, 0:L - 1], cw(1), xc[:, 1:L], ALU.mult, ALU.add, ['F0', 'F2'], ['F2'])
                    self.stt(xc[:, 0:L - 1], xa[:, 1:L], cw(3), xc[:, 0:L - 1], ALU.mult, ALU.add, ['F0', 'F2'], ['F2'])
                    self.cp(G[0][:, :], xc[:, :], ['F2'], ['G0'], eng='pool')
                    for d in range(2):
                        for gi, (dst, key, bofs) in enumerate(((Fr, 'F4', PP['br']), (Fi, 'F5', PP['bi']))):
                            bias = ppt[:, bofs + d * 4 + ct:bofs + d * 4 + ct + 1]
                            wt = bd[:, (gi * 2 + d) * 4 + ct, :]
                            for (t0, n) in TOKB:
                                b = S.bank()
                                self.mm(self.ps[b][:, 0:n], [(wt, G[0][:, t0:t0 + n])], ['bd', 'G0'], [f'ps{b}'])
                                self.act(dst[:, t0:t0 + n], self.ps[b][:, 0:n], AF.Sigmoid, [f'ps{b}', 'ppt'], [key], bias=bias)
                        k8 = d * 4 + ct
                        self.act(Fr[:, :], Fr[:, :], AF.Exp, ['F4', 'sc8'], ['F4'], scale=sc8[:, k8:k8 + 1])
                        self.tt(Fi[:, :], Fi[:, :], xc[:, :], ALU.mult, ['F5', 'F2'], ['F5'])
                        self.tt(Ft[:, :], Fr[:, :], Fr[:, :], ALU.mult, ['F4'], ['F6'], eng='pool')
                        self.act(Ft[:, :], Ft[:, :], AF.Sqrt, ['F6', 'one'], ['F6'], bias=one[:, :], scale=-1.0)
                        self.tt(Fi[:, :], Fi[:, :], Ft[:, :], ALU.mult, ['F5', 'F6'], ['F5'])
                        if d == 0:
                            S.op('dve', lambda e: e.tensor_tensor_scan(out=hsum[:, :], data0=Fr[:, :], data1=Fi[:, :], initial=0.0,
                                                                       op0=ALU.mult, op1=ALU.add), r=['F4', 'F5'], w=['F3'])
                        else:
                            S.op('dve', lambda e: e.tensor_tensor_scan(out=Ft[:, ::-1], data0=Fr[:, ::-1], data1=Fi[:, ::-1], initial=0.0,
                                                                       op0=ALU.mult, op1=ALU.add), r=['F4', 'F5'], w=['F6'])
                            self.tt(hsum[:, :], hsum[:, :], Ft[:, :], ALU.add, ['F3', 'F6'], ['F3'])
                    self.tt(G[1][:, :], gate[:, :], hsum[:, :], ALU.mult, ['F1', 'F3'], ['G1'])
                    self.ld(self.OTc[ct * 128:(ct + 1) * 128, :], G[1][:, :], ['G1'], ['OTc'])
                self.ld(OT[s][0], self.OTc, ['OTc'], ['OT'])


def pack_pp(inp):
    pp = np.zeros((2, 128, NPP), np.float32)
    for li in range(2):
        pp[li, :, PP['cw']:PP['cw'] + 16] = inp['conv_w'][li].reshape(4, 4, 128).transpose(2, 1, 0).reshape(128, 16)
        pp[li, :, PP['cb']:PP['cb'] + 4] = inp['conv_b'][li].reshape(4, 128).T
        pp[li, :, PP['br']:PP['br'] + 8] = inp['lru_br'][li].reshape(2, 4, 128).transpose(2, 0, 1).reshape(128, 8)
        pp[li, :, PP['bi']:PP['bi'] + 8] = inp['lru_bi'][li].reshape(2, 4, 128).transpose(2, 0, 1).reshape(128, 8)
        pp[li, :, PP['lam']:PP['lam'] + 8] = inp['lru_lambda'][li].reshape(2, 4, 128).transpose(2, 0, 1).reshape(128, 8)
        pp[li, :, PP['lb0']:PP['lb0'] + 4] = inp['hgrn_lb'][0].reshape(4, 128).T
        pp[li, :, PP['lb1']:PP['lb1'] + 4] = inp['hgrn_lb'][1].reshape(4, 128).T
        pp[li, :, PP['ng']:PP['ng'] + 4] = inp['hgrn_norm_g'][li].reshape(4, 128).T
        rd = inp['ret_decay'][li]
        for d in range(2):
            for th in range(2):
                pp[li, 0:64, PP['rdec'] + d * 2 + th] = rd[d, th * 2]
                pp[li, 64:128, PP['rdec'] + d * 2 + th] = rd[d, th * 2 + 1]
    return pp


def make_lru_bd(inp):
    bd = np.zeros((2, 16, 128, 128), np.float32)
    for li in range(2):
        for gi, nm in enumerate(('lru_wr', 'lru_wi')):
            for d in range(2):
                for ct in range(4):
                    g = (gi * 2 + d) * 4 + ct
                    bd[li, g, 0:64, 0:64] = inp[nm][li, d, 2 * ct]
                    bd[li, g, 64:128, 64:128] = inp[nm][li, d, 2 * ct + 1]
    return bd


def host_consts():
    c = {}
    c['c_ident'] = np.eye(128, dtype=np.float32)
    p = np.arange(128)
    sidx = (p % 64)[:, None]
    tt_ = np.arange(64)[None, :]
    c['c_maskf'] = np.tile((sidx <= tt_).astype(np.float32)[:, None, :], (1, 4, 1)).reshape(128, 256)
    c['c_maskb'] = np.tile((sidx >= tt_).astype(np.float32)[:, None, :], (1, 4, 1)).reshape(128, 256)
    cc = p % 64
    inv = 10000.0 ** (-(cc % 32).astype(np.float64) / 32.0)
    ang = np.arange(L, dtype=np.float64)[None, :] * inv[:, None]
    c['c_cos'] = np.cos(ang).astype(np.float32)
    c['c_sin'] = (np.where(cc < 32, -1.0, 1.0)[:, None] * np.sin(ang)).astype(np.float32)
    c['c_iota'] = np.stack([np.arange(L), L - 1 - np.arange(L)]).astype(np.float32)
    c['c_triu'] = (p[:, None] < p[None, :]).astype(np.float32)
    c['c_trash'] = np.tile((CAPL + p).astype(np.float32)[:, None], (1, 16))
    c['c_iotaw'] = np.tile(np.arange(40, dtype=np.float32)[None, :], (128, 1))
    return c


def make_w_sw(inp):
    w = inp['w_in']
    out = np.zeros((2, D, 512), np.float32)
    for part, c0 in ((0, COL['dq']), (1, COL['dk'])):
        blk = w[:, :, c0:c0 + 256].reshape(2, D, 4, 2, 32)
        out[:, :, part * 256:(part + 1) * 256] = blk[:, :, :, ::-1, :].reshape(2, D, 256)
    return out


def _gla_methods():
    def gla_tile(self, j, d, kmap, qh, kh, Vtok, Zf, Zb, E4, mask, osum, T):
        S = self.S
        nk = kh.shape[1]
        t0 = toff(j)
        rows = trows(j)
        cw = 64 if j > 0 else 16
        ncj = 2 if j > 0 else 1
        i2 = j % 2
        bA = S.bank()
        psA = self.ps[bA][:, 0:256].rearrange("p (h t) -> p h t", h=4)
        groups = []
        for cj in range(ncj):
            c0 = t0 + cw * cj
            for h in range(4):
                groups.append((psA[cw * cj:cw * cj + cw, h, 0:cw], [(kh[:, kmap[h], c0:c0 + cw], qh[:, h, c0:c0 + cw])]))
        self.mms(groups, ['kh', 'qh'], [f'ps{bA}'])
        atf, at, kt = T['atf'], T['at'][i2], T['kt'][i2]
        self.ts(atf[0:rows, :, 0:cw], psA[0:rows, :, 0:cw], -BIGF, BIGF, ALU.max, ALU.min, [f'ps{bA}'], ['atf'])
        for cj in range(ncj):
            pr = slice(cw * cj, cw * cj + cw)
            self.tt(at[cj][pr, :, 0:cw], atf[pr, :, 0:cw], mask[pr, :, 0:cw], ALU.mult, ['atf', 'mask'], [f'at{i2}{cj}'],
                    eng=('dve' if cj == 0 else 'pool'))
        bK = S.bank()
        psK = self.ps[bK][:, 0:nk * 128].rearrange("p (n c) -> p n c", n=nk)
        groups = [(psK[0:rows, n, :], [(kh[:, n, t0:t0 + rows], self.c['identb'][:, :])]) for n in range(nk)]
        self.mms(groups, ['kh', 'identb'], [f'ps{bK}'])
        for cj in range(ncj):
            pr = slice(cw * cj, cw * cj + cw)
            self.cp(kt[cj][pr, :, :], psK[pr, :, :], [f'ps{bK}'], [f'kt{i2}{cj}'], eng='act')
        bO = S.bank()
        psO = self.ps[bO][:, :].rearrange("p (c h t) -> p c h t", c=2, h=4)
        order = list(range(ncj)) if d == 0 else list(range(ncj))[::-1]
        for cj in order:
            cid = 0 if j == 0 else 1 + 2 * (j - 1) + cj
            c0 = t0 + cw * cj
            groups = []
            for h in range(4):
                groups.append((psO[:, cj, h, 0:cw], [(Zb[:, h, :], qh[:, h, c0:c0 + cw]),
                                                     (Vtok[:, j, h * 128:(h + 1) * 128], at[cj][:, h, 0:cw])]))
            self.mms(groups, ['Zb', 'qh', 'Vtok', f'at{i2}{cj}'], [f'ps{bO}'])
            bS = S.bank()
            psS = self.ps[bS][:, :].rearrange("p (h v) -> p h v", h=4)
            groups = [(psS[:, h, :], [(kt[cj][:, kmap[h], :], Vtok[:, j, h * 128:(h + 1) * 128])]) for h in range(4)]
            self.mms(groups, [f'kt{i2}{cj}', 'Vtok'], [f'ps{bS}'])
            self.tt(Zf[:], Zf[:], psS, ALU.add, ['Zf', f'ps{bS}'], ['Zf'])
            self.tt(Zf[:], Zf[:], E4[:, :, cid:cid + 1].to_broadcast([128, 4, 128]), ALU.mult, ['Zf', 'E4'], ['Zf'], eng='pool')
            self.cp(Zb[:], Zf[:], ['Zf'], ['Zb'], eng='act')
        if j > 0:
            dst = osum[:, :, t0:t0 + 128].rearrange("p h (c t) -> p c h t", c=2)
            src = psO
        else:
            dst = osum[:, :, 0:16]
            src = psO[:, 0, :, 0:16]
        if d == 0:
            self.cp(dst, src, [f'ps{bO}'], ['osum'])
        else:
            self.tt(dst, dst, src, ALU.add, ['osum', f'ps{bO}'], ['osum'])

    def gla_dir(self, d, kmap, qh, kh, Vtok, osum, Ref, T, maskf, maskb):
        S = self.S
        nk = kh.shape[1]
        E, E4, Zf, Zb = T['E'], T['E4'], T['Zf'], T['Zb']
        self.ms(E[:], 0.0, ['E'])
        if d == 0:
            self.tt(E[:, :, 0:32], Ref[:, :, 1:33], Ref[:, :, 0:32], ALU.subtract, ['Ref', 'E'], ['E'])
        else:
            self.tt(E[:, :, 1:33], Ref[:, :, 0:32], Ref[:, :, 1:33], ALU.subtract, ['Ref', 'E'], ['E'])
        self.act(E[:], E[:], AF.Exp, ['E'], ['E'])
        for h in range(4):
            self.cp(E4[:, h, :], E[:, kmap[h], :], ['E'], ['E4'], eng='pool')
        self.ms(Zf[:], 0.0, ['Zf'])
        self.ms(Zb[:], 0.0, ['Zb'])
        tiles = list(range(NT)) if d == 0 else list(range(NT))[::-1]
        for j in tiles:
            self.gla_tile(j, d, kmap, qh, kh, Vtok, Zf, Zb, E4, maskf if d == 0 else maskb, osum, T)

    def gla_norm_dk(self, B, th, kfp, qsrc, qscale, qdst, kdst, Ref, Tt, r_q):
        T3, T4 = Tt[2], Tt[3]
        self.cp(Ref[:, th, 0:1], T3[:, 7:8], ['T2'], ['Ref'], eng='pool')
        self.cp(Ref[:, th, 1:33], T3[:, 47:L:64], ['T2'], ['Ref'], eng='pool')
        main = T3[:, 16:L].rearrange("p (m c) -> p m c", c=64)
        self.tt(main, main, Ref[:, th, 1:33].unsqueeze(2).to_broadcast([128, 32, 64]), ALU.subtract, ['T2', 'Ref'], ['T2'])
        self.tt(T3[:, 0:16], T3[:, 0:16], Ref[:, th, 0:1].to_broadcast([128, 16]), ALU.subtract, ['T2', 'Ref'], ['T2'])
        self.act(T4[:, :], T3[:, :], AF.Exp, ['T2'], ['T3'])
        for (dst, pr) in qdst:
            self.stt(dst[pr, :], qsrc[pr, :], qscale, T4[pr, :], ALU.mult, ALU.mult, ['T3'] + r_q, ['qh'])
        self.act(T4[:, :], T3[:, :], AF.Exp, ['T2', 'T3', 'qh'], ['T3'], scale=-1.0)
        self.tt(kdst, kfp, T4[:, :], ALU.mult, ['T3', 'T0', 'kb'], ['kh'])

    def gla_alloc(self, nk):
        T = {}
        T['atf'] = self.tile("atf", [128, 4, 64])
        T['at'] = [[self.tile(f"at{i}{c}", [128, 4, 64], BF16) for c in range(2)] for i in range(2)]
        T['kt'] = [[self.tile(f"kt{i}{c}", [128, nk, 128], BF16) for c in range(2)] for i in range(2)]
        for i in range(2):
            for c in range(2):
                self.ms(T['at'][i][c][:], 0.0, [f'at{i}{c}'])
                self.ms(T['kt'][i][c][:], 0.0, [f'kt{i}{c}'])
        T['E'] = self.tile("E", [128, nk, 33])
        T['E4'] = self.tile("E4", [128, 4, 33])
        T['Zf'] = self.tile("Zf", [128, 4, 128])
        T['Zb'] = self.tile("Zb", [128, 4, 128], BF16)
        return T

    def proj_vtok(self, wb, wkey, hT, Vtok):
        for j in range(NT):
            rows, t0 = trows(j), toff(j)
            b = self.S.bank()
            self.mm(self.ps[b][0:rows, :], [(hT[:, k, t0:t0 + rows], wb[:, k, 0:512]) for k in range(8)], ['hT', wkey], [f'ps{b}'])
            self.cp(Vtok[0:rows, j, :], self.ps[b][0:rows, :], [f'ps{b}'], ['Vtok'], eng=('act' if j % 2 else 'dve'))

    Builder.gla_tile = gla_tile
    Builder.gla_dir = gla_dir
    Builder.gla_norm_dk = gla_norm_dk
    Builder.gla_alloc = gla_alloc
    Builder.proj_vtok = proj_vtok


_gla_methods()


def _gla_phases():
    def gla_common(self, nk):
        self.load_consts()
        P = {}
        P['hT'] = self.tile("hT", [128, 8, L], BF16)
        self.tl_hx = [self.tile(f"hx{i}", [128, D]) for i in range(2)]
        P['wb'] = self.tile("wb", [128, 8, 512], BF16)
        P['Tt'] = [self.tile(f"T{i}", [128, L]) for i in range(4)]
        P['qh'] = self.tile("qh", [128, 4, L], BF16)
        P['kh'] = self.tile("kh", [128, nk, L], BF16)
        P['osum'] = self.tile("osum", [128, 4, L])
        P['Vtok'] = self.tile("Vtok", [128, NT, 512], BF16)
        P['Ref'] = self.tile("Ref", [128, nk, 33])
        P['T'] = self.gla_alloc(nk)
        P['maskf'] = self.tile("maskf", [128, 4, 64])
        P['maskb'] = self.tile("maskb", [128, 4, 64])
        self.ld(P['maskf'][:], self.ins['c_maskf'].rearrange("p (h t) -> p h t", h=4), [], ['mask'])
        self.ld(P['maskb'][:], self.ins['c_maskb'].rearrange("p (h t) -> p h t", h=4), [], ['mask'])
        P['ppt'] = self.tile("ppt", [128, NPP])
        P['onesM'] = self.tile("onesM", [128, 128])
        self.ms(P['onesM'][:], 1.0 / 128.0, ['onesM'])
        P['eps6'] = self.tile("eps6", [128, 1])
        self.ms(P['eps6'][:], 1e-6, ['eps6'])
        return P

    def phase_C(self, li, H, WIN, OT):
        S = self.S
        with self.scope():
            P = self.gla_common(4)
            hT, wb, Tt, qh, kh, osum, Vtok, Ref, ppt = (P[k] for k in ('hT', 'wb', 'Tt', 'qh', 'kh', 'osum', 'Vtok', 'Ref', 'ppt'))
            T0, T1, T2, T3 = Tt
            self.ld(ppt[:], self.ins['pp'][li], [], ['ppt'])
            qb = self.tile("qb", [128, 4, L], BF16)
            lbv = self.tile("lbv", [128, 4])
            oml = self.tile("oml", [128, 4])
            if li == 0:
                self.ms(lbv[:], 0.0, ['lbv'])
                self.ms(oml[:], 1.0, ['lbv'])
            else:
                self.tt(lbv[:], ppt[:, PP['lb1']:PP['lb1'] + 4], ppt[:, PP['lb0']:PP['lb0'] + 4], ALU.subtract, ['ppt'], ['lbv'])
                self.act(lbv[:], lbv[:], AF.Sigmoid, ['lbv'], ['lbv'])
                self.ts(oml[:], lbv[:], -1.0, 1.0, ALU.mult, ALU.add, ['lbv'], ['lbv'])
            self.dq = 'act'
            for s in S.iter(self.nseq, False):
                self.build_hT(H, s, hT, self.c['ident'])
                self.load_wblk(wb, 'wb', WIN, li, COL['cq'], 512)
                for th in range(4):
                    self.proj_fm(wb, 'wb', th * 128, 128, hT,
                                 lambda ps, t0, n, bk, th=th: self.act(qb[:, th, t0:t0 + n], ps, AF.Silu, [bk], ['qb']))
                self.load_wblk(wb, 'wb', WIN, li, COL['ci'], 512)
                self.ms(Vtok[:, 0, :], 0.0, ['Vtok'])
                self.proj_vtok(wb, 'wb', hT, Vtok)
                for d in range(2):
                    self.load_wblk(wb, 'wb', WIN, li, COL['cf'] + d * 512, 512)
                    for th in range(4):
                        self.proj_fm(wb, 'wb', th * 128, 128, hT,
                                     lambda ps, t0, n, bk: self.act(T0[:, t0:t0 + n], ps, AF.Sigmoid, [bk], ['T0']))
                        self.ts(T0[:, :], T0[:, :], oml[:, th:th + 1], lbv[:, th:th + 1], ALU.mult, ALU.add, ['T0', 'lbv'], ['T0'])
                        self.act(T1[:, :], T0[:, :], AF.Ln, ['T0'], ['T1'])
                        self.ts(T0[:, :], T0[:, :], -1.0, 1.0, ALU.mult, ALU.add, ['T0', 'T1'], ['T0'], eng='pool')
                        self.ms(T3[:, :], 1.0, ['T3'])
                        if d == 0:
                            S.op('dve', lambda e: e.tensor_tensor_scan(out=T2[:, :], data0=T3[:, :], data1=T1[:, :], initial=0.0,
                                                                       op0=ALU.mult, op1=ALU.add), r=['T3', 'T1'], w=['T2'])
                        else:
                            S.op('dve', lambda e: e.tensor_tensor_scan(out=T2[:, ::-1], data0=T3[:, ::-1], data1=T1[:, ::-1], initial=0.0,
                                                                       op0=ALU.mult, op1=ALU.add), r=['T3', 'T1'], w=['T2'])
                        self.gla_norm_dk(T2, th, T0[:, :], qb[:, th, :], 128.0 ** -0.5, [(qh[:, th, :], slice(0, 128))], kh[:, th, :],
                                         Ref, Tt, ['qb'])
                    self.gla_dir(d, [0, 1, 2, 3], qh, kh, Vtok, osum, Ref, P['T'], P['maskf'], P['maskb'])
                self.load_wblk(wb, 'wb', WIN, li, COL['cg'], 512)
                for h in range(4):
                    self.act(T0[:, :], osum[:, h, :], AF.Square, ['osum'], ['T0'])
                    for (t0, n) in TOKB:
                        b = S.bank()
                        self.mm(self.ps[b][:, 0:n], [(P['onesM'][:, :], T0[:, t0:t0 + n])], ['onesM', 'T0'], [f'ps{b}'])
                        self.act(T1[:, t0:t0 + n], self.ps[b][:, 0:n], AF.Sqrt, [f'ps{b}', 'eps6'], ['T1'], bias=P['eps6'][:, :])
                    S.op('dve', lambda e: e.reciprocal(out=T1[:, :], in_=T1[:, :]), r=['T1'], w=['T1'])
                    self.tt(T2[:, :], osum[:, h, :], T1[:, :], ALU.mult, ['osum', 'T1'], ['T2'])
                    self.proj_fm(wb, 'wb', h * 128, 128, hT,
                                 lambda ps, t0, n, bk: self.act(T3[:, t0:t0 + n], ps, AF.Silu, [bk], ['T3']))
                    ng = ppt[:, PP['ng'] + h:PP['ng'] + h + 1]
                    self.stt(qh[:, h, :], T2[:, :], ng, T3[:, :], ALU.mult, ALU.mult, ['T2', 'T3', 'ppt'], ['qh'])
                    self.ld(self.OTc[h * 128:(h + 1) * 128, :], qh[:, h, :], ['qh'], ['OTc'])
                self.ld(OT[s][2], self.OTc, ['OTc'], ['OT'], q=self.dq)

    def phase_D(self, li, H, WIN, WSW, OT):
        S = self.S
        with self.scope():
            P = self.gla_common(2)
            hT, wb, Tt, qh, kh, osum, Vtok, Ref, ppt = (P[k] for k in ('hT', 'wb', 'Tt', 'qh', 'kh', 'osum', 'Vtok', 'Ref', 'ppt'))
            T0, T1, T2, T3 = Tt
            self.ld(ppt[:], self.ins['pp'][li], [], ['ppt'])
            wb2 = self.tile("wb2", [128, 8, 512], BF16)
            qb = self.tile("qb", [128, 2, L], BF16)
            kb = self.tile("kb", [128, 2, L], BF16)
            cosT = self.tile("cosT", [128, L])
            sinT = self.tile("sinT", [128, L])
            self.ld(cosT[:], self.ins['c_cos'], [], ['rot'])
            self.ld(sinT[:], self.ins['c_sin'], [], ['rot'])
            lg = self.tile("lg", [128, 4])
            self.act(lg[:], ppt[:, PP['rdec']:PP['rdec'] + 4], AF.Exp, ['ppt'], ['lg'], scale=-1.0)
            self.ts(lg[:], lg[:], 1.0, None, ALU.add, None, ['lg'], ['lg'])
            self.act(lg[:], lg[:], AF.Ln, ['lg'], ['lg'])
            self.ts(lg[:], lg[:], -1.0, None, ALU.mult, None, ['lg'], ['lg'])
            self.ms(qh[:], 0.0, ['qh'])
            self.load_wblk(wb2, 'wb2', WSW, li, 0, 512)
            self.dq = 'act'
            for s in S.iter(self.nseq, False):
                self.build_hT(H, s, hT, self.c['ident'])
                self.load_wblk(wb, 'wb', WIN, li, COL['dq'], 512)
                for (dst, dkey, c0, sc) in ((qb, 'qb', 0, 1.0), (kb, 'kb', 256, 0.125)):
                    for th in range(2):
                        self.proj_fm(wb, 'wb', c0 + th * 128, 128, hT,
                                     lambda ps, t0, n, bk, sc=sc: self.act(T0[:, t0:t0 + n], ps, AF.Copy, [bk], ['T0'], scale=sc))
                        self.proj_fm(wb2, 'wb2', c0 + th * 128, 128, hT,
                                     lambda ps, t0, n, bk, sc=sc: self.act(T1[:, t0:t0 + n], ps, AF.Copy, [bk], ['T1'], scale=sc))
                        self.tt(T0[:, :], T0[:, :], cosT[:, :], ALU.mult, ['T0', 'rot'], ['T0'])
                        self.tt(T1[:, :], T1[:, :], sinT[:, :], ALU.mult, ['T1', 'rot'], ['T1'], eng='pool')
                        self.tt(dst[:, th, :], T0[:, :], T1[:, :], ALU.add, ['T0', 'T1'], [dkey])
                self.load_wblk(wb, 'wb', WIN, li, COL['dv'], 512)
                self.ms(Vtok[:, 0, :], 0.0, ['Vtok'])
                self.proj_vtok(wb, 'wb', hT, Vtok)
                for d in range(2):
                    self.ld(T1[:, :], self.ins['c_iota'][d].partition_broadcast(128), [], ['T1'])
                    for th in range(2):
                        self.ts(T2[:, :], T1[:, :], lg[:, d * 2 + th:d * 2 + th + 1], None, ALU.mult, None, ['T1', 'lg'], ['T2'])
                        self.gla_norm_dk(T2, th, kb[:, th, :], qb[:, th, :], 1.0,
                                         [(qh[:, 2 * th, :], slice(0, 64)), (qh[:, 2 * th + 1, :], slice(64, 128))], kh[:, th, :],
                                         Ref, Tt, ['qb'])
                    self.gla_dir(d, [0, 0, 1, 1], qh, kh, Vtok, osum, Ref, P['T'], P['maskf'], P['maskb'])
                self.load_wblk(wb, 'wb', WIN, li, COL['dg'], 512)
                for h in range(4):
                    for (t0, n) in TOKB:
                        b = S.bank()
                        self.mm(self.ps[b][:, 0:n], [(P['onesM'][:, :], osum[:, h, t0:t0 + n])], ['onesM', 'osum'], [f'ps{b}'])
                        self.tt(T0[:, t0:t0 + n], osum[:, h, t0:t0 + n], self.ps[b][:, 0:n], ALU.subtract, ['osum', f'ps{b}'], ['T0'])
                    self.act(T1[:, :], T0[:, :], AF.Square, ['T0'], ['T1'])
                    for (t0, n) in TOKB:
                        b = S.bank()
                        self.mm(self.ps[b][:, 0:n], [(P['onesM'][:, :], T1[:, t0:t0 + n])], ['onesM', 'T1'], [f'ps{b}'])
                        self.act(T2[:, t0:t0 + n], self.ps[b][:, 0:n], AF.Sqrt, [f'ps{b}', 'eps6'], ['T2'], bias=P['eps6'][:, :])
                    S.op('dve', lambda e: e.reciprocal(out=T2[:, :], in_=T2[:, :]), r=['T2'], w=['T2'])
                    self.tt(T0[:, :], T0[:, :], T2[:, :], ALU.mult, ['T0', 'T2'], ['T0'])
                    self.proj_fm(wb, 'wb', h * 128, 128, hT,
                                 lambda ps, t0, n, bk: self.act(T3[:, t0:t0 + n], ps, AF.Silu, [bk], ['T3']))
                    self.tt(kh[:, h % 2, :], T0[:, :], T3[:, :], ALU.mult, ['T0', 'T3'], ['kh'])
                    self.ld(self.OTc[h * 128:(h + 1) * 128, :], kh[:, h % 2, :], ['kh'], ['OTc'])
                self.ld(OT[s][3], self.OTc, ['OTc'], ['OT'], q=self.dq)

    Builder.gla_common = gla_common
    Builder.phase_C = phase_C
    Builder.phase_D = phase_D


_gla_phases()


def _rwkv_phases():
    TB = 16
    NBLK = L // TB

    def phase_B1(self, li, H, WMIX, SCANIN, BONUS, GATE):
        S = self.S
        with self.scope():
            self.load_consts()
            hT = self.tile("hT", [128, 8, L], BF16)
            self.tl_hx = [self.tile(f"hx{i}", [128, D]) for i in range(2)]
            wm = self.tile("wm", [128, 16, 1920], BF16)
            self.ld(wm[:], WMIX[li].rearrange("(k p) n -> p k n", p=128), [], ['wm'])
            hsh = self.tile("hsh", [128, 8, 512], BF16)
            lo3 = self.tile("lo3", [128, 3, 512], BF16)
            names = ['w0_0', 'w0_1', 'a0_0', 'a0_1', 'kk', 'ka', 'omka', 'rk']
            bc = {n: self.tile("bc_" + n, [128, 512]) for n in names}
            self.ld(bc['w0_0'][:], self.ins['rwkv_w0'][li, 0].partition_broadcast(128), [], ['bc'])
            self.ld(bc['w0_1'][:], self.ins['rwkv_w0'][li, 1].partition_broadcast(128), [], ['bc'])
            self.ld(bc['a0_0'][:], self.ins['rwkv_a0'][li, 0].partition_broadcast(128), [], ['bc'])
            self.ld(bc['a0_1'][:], self.ins['rwkv_a0'][li, 1].partition_broadcast(128), [], ['bc'])
            self.ld(bc['kk'][:], self.ins['rwkv_kk'][li].partition_broadcast(128), [], ['bc'])
            self.ld(bc['ka'][:], self.ins['rwkv_ka'][li].partition_broadcast(128), [], ['bc'])
            self.ld(bc['rk'][:], self.ins['rwkv_rk'][li].partition_broadcast(128), [], ['bc'])
            self.ts(bc['omka'][:], bc['ka'][:], -1.0, 1.0, ALU.mult, ALU.add, ['bc'], ['bc'])
            l32 = self.tile("l32", [128, 3, 512])
            lw = self.tile("lw", [128, 3, 512], BF16)
            self.ld(l32[:, 0, :], self.ins['rwkv_w2'][li].rearrange("d r c -> (d r) c"), [], ['l32'])
            self.ld(l32[:, 1, :], self.ins['rwkv_a2'][li].rearrange("d r c -> (d r) c"), [], ['l32'])
            self.ld(l32[:, 2, :], self.ins['rwkv_g2'][li], [], ['l32'])
            self.cp(lw[:], l32[:], ['l32'], ['lw'])
            SH = self.tile("SH", [128, 3, 512])
            PD = [self.tile(f"PD{d}", [128, 3, 512]) for d in range(2)]
            kt = self.tile("ktk", [128, 512])
            U = [self.tile(f"U{i}", [128, 512]) for i in range(5)]
            sm = self.tile("sm", [128, 32])
            SCc, BONc, GATc = self.SCc, self.BONc, self.GATc
            self.dq = 'sp'
            for s in S.iter(self.nseq, False):
                self.build_hT(H, s, hT, self.c['ident'])
                for (t0, n) in TOKB:
                    a = 1 if t0 == 0 else 0
                    b_ = n - 1 if t0 + n == L else n
                    self.tt(hsh[:, :, a:b_], hT[:, :, t0 + a - 1:t0 + b_ - 1], hT[:, :, t0 + a + 1:t0 + b_ + 1], ALU.add, ['hT'], ['hsh'])
                    if a == 1:
                        self.cp(hsh[:, :, 0:1], hT[:, :, 1:2], ['hT'], ['hsh'])
                    if b_ == n - 1:
                        self.cp(hsh[:, :, n - 1:n], hT[:, :, L - 2:L - 1], ['hT'], ['hsh'])
                    for q, fn in enumerate((AF.Tanh, AF.Copy, AF.Sigmoid)):
                        bk = S.bank()
                        c0 = 1536 + q * 128
                        pairs = [(wm[:, k, c0:c0 + 128], hT[:, k, t0:t0 + n]) for k in range(8)] + \
                                [(wm[:, 8 + k, c0:c0 + 128], hsh[:, k, 0:n]) for k in range(8)]
                        self.mm(self.ps[bk][:, 0:n], pairs, ['wm', 'hT', 'hsh'], [f'ps{bk}'])
                        self.act(lo3[:, q, 0:n], self.ps[bk][:, 0:n], fn, [f'ps{bk}'], ['lo3'])
                    jl = [j for j in range(NT) if t0 <= toff(j) < t0 + n]
                    for j in jl:
                        rows, lo = trows(j), toff(j) - t0
                        tg = toff(j)
                        pss = []
                        for q in range(3):
                            bk = S.bank()
                            pairs = [(hT[:, k, tg:tg + rows], wm[:, k, q * 512:(q + 1) * 512]) for k in range(8)] + \
                                    [(hsh[:, k, lo:lo + rows], wm[:, 8 + k, q * 512:(q + 1) * 512]) for k in range(8)]
                            self.mm(self.ps[bk][0:rows, :], pairs, ['wm', 'hT', 'hsh'], [f'ps{bk}'])
                            pss.append(bk)
                        R = slice(0, rows)
                        self.act(SH[R, 1, :], self.ps[pss[0]][R, :], AF.Copy, [f'ps{pss[0]}'], ['SH'])
                        self.cp(kt[R, :], self.ps[pss[1]][R, :], [f'ps{pss[1]}'], ['kt'])
                        self.act(SH[R, 2, :], self.ps[pss[2]][R, :], AF.Copy, [f'ps{pss[2]}'], ['SH'])
                        self.tt(U[0][R, :], kt[R, :], bc['kk'][R, :], ALU.mult, ['kt', 'bc'], ['U0'])
                        self.act(U[1][R, :], U[0][R, :], AF.Square, ['U0'], ['U1'])
                        S.op('dve', lambda e, R=R: e.tensor_reduce(out=sm[R, 0:8], in_=U[1][R, :].rearrange("p (h j) -> p h j", h=8),
                                                                op=ALU.add, axis=AX.X), r=['U1'], w=['sm'])
                        self.act(sm[R, 0:8], sm[R, 0:8], AF.Sqrt, ['sm'], ['sm'])
                        self.ts(sm[R, 0:8], sm[R, 0:8], 1e-12, None, ALU.max, None, ['sm'], ['sm'])
                        S.op('dve', lambda e, R=R: e.reciprocal(out=sm[R, 8:16], in_=sm[R, 0:8]), r=['sm'], w=['sm'])
                        self.tt(SH[R, 0, :].rearrange("p (h j) -> p h j", h=8), U[0][R, :].rearrange("p (h j) -> p h j", h=8),
                                sm[R, 8:16].unsqueeze(2).to_broadcast([rows, 8, 64]), ALU.mult, ['U0', 'sm'], ['SH'])
                        for d in range(2):
                            bk = S.bank()
                            self.mm(self.ps[bk][R, :], [(lo3[64 * d:64 * d + 64, 0, lo:lo + rows], lw[64 * d:64 * d + 64, 0, :])],
                                    ['lo3', 'lw'], [f'ps{bk}'])
                            self.tt(U[2][R, :], self.ps[bk][R, :], bc[f'w0_{d}'][R, :], ALU.add, [f'ps{bk}', 'bc'], ['U2'])
                            self.act(U[2][R, :], U[2][R, :], AF.Sigmoid, ['U2'], ['U2'])
                            self.act(PD[d][R, 0, :], U[2][R, :], AF.Exp, ['U2'], [f'PD{d}'], scale=-0.6065306597126334)
                            bk = S.bank()
                            self.mm(self.ps[bk][R, :], [(lo3[64 * d:64 * d + 64, 1, lo:lo + rows], lw[64 * d:64 * d + 64, 1, :])],
                                    ['lo3', 'lw'], [f'ps{bk}'])
                            self.tt(U[3][R, :], self.ps[bk][R, :], bc[f'a0_{d}'][R, :], ALU.add, [f'ps{bk}', 'bc'], ['U3'])
                            self.act(U[3][R, :], U[3][R, :], AF.Sigmoid, ['U3'], ['U3'])
                            self.tt(PD[d][R, 1, :], SH[R, 0, :], U[3][R, :], ALU.mult, ['SH', 'U3'], [f'PD{d}'], eng='pool')
                            self.tt(U[4][R, :], U[3][R, :], bc['ka'][R, :], ALU.mult, ['U3', 'bc'], ['U4'])
                            self.tt(U[4][R, :], U[4][R, :], bc['omka'][R, :], ALU.add, ['U4', 'bc'], ['U4'], eng='pool')
                            self.tt(PD[d][R, 2, :], U[4][R, :], kt[R, :], ALU.mult, ['U4', 'kt'], [f'PD{d}'])
                        self.tt(U[1][R, :], PD[0][R, 2, :], PD[1][R, 2, :], ALU.add, ['PD0', 'PD1'], ['U1'], eng='pool')
                        self.tt(U[1][R, :], U[1][R, :], SH[R, 1, :], ALU.mult, ['U1', 'SH'], ['U1'])
                        self.tt(U[1][R, :], U[1][R, :], bc['rk'][R, :], ALU.mult, ['U1', 'bc'], ['U1'], eng='pool')
                        S.op('dve', lambda e, R=R: e.tensor_reduce(out=sm[R, 16:24], in_=U[1][R, :].rearrange("p (h j) -> p h j", h=8),
                                                                op=ALU.add, axis=AX.X), r=['U1'], w=['sm2'])
                        self.tt(U[0][R, :].rearrange("p (h j) -> p h j", h=8), SH[R, 2, :].rearrange("p (h j) -> p h j", h=8),
                                sm[R, 16:24].unsqueeze(2).to_broadcast([rows, 8, 64]), ALU.mult, ['SH', 'sm2', 'U0'], ['U0'])
                        self.ld(BONc[tg:tg + rows, :], U[0][R, :], ['U0'], ['BONc'])
                        bk = S.bank()
                        self.mm(self.ps[bk][R, :], [(lo3[:, 2, lo:lo + rows], lw[:, 2, :])], ['lo3', 'lw'], [f'ps{bk}'])
                        self.act(U[2][R, :], self.ps[bk][R, :], AF.Copy, [f'ps{bk}', 'U2'], ['U2'])
                        self.ld(GATc[tg:tg + rows, :], U[2][R, :], ['U2'], ['GATc'])
                        for d in range(2):
                            self.ld(SCc[d, tg:tg + rows, 0:3, :], SH[R, :, :], ['SH'], ['SCc'])
                            self.ld(SCc[d, tg:tg + rows, 3:6, :], PD[d][R, :, :], [f'PD{d}'], ['SCc'])
                for d in range(2):
                    self.ld(SCANIN[d][s], SCc[d], ['SCc'], ['SCANIN'])
                self.ld(BONUS[s], BONc, ['BONc'], ['BONUS'])
                self.ld(GATE[s], GATc, ['GATc'], ['GATE'])

    def phase_B2(self, SCANIN, YSCAN):
        S = self.S
        ns = self.nseq
        NP = 2 * ns * 8
        XST, YST = self.XST, self.YST
        with self.scope():
            X = self.tile("X", [NP, TB, 6, 64])
            St = self.tile("St", [NP, 64, 64])
            tmp = self.tile("tmp", [NP, 64, 64])
            Y = self.tile("Y", [NP, TB, 64])
            sa = self.tile("sa", [NP, 64])
            self.ms(St[:], 0.0, ['St'])
            sc0 = SCANIN[0].rearrange("s (b t) c x -> s b (t c x)", t=TB)
            sc1 = SCANIN[1].rearrange("s (b t) c x -> s b (t c x)", t=TB)
            ys0 = YSCAN[0].rearrange("s (b t) x -> s b (t x)", t=TB)
            ys1 = YSCAN[1].rearrange("s (b t) x -> s b (t x)", t=TB)
            with S.loop(NBLK) as blk:
                rb = (NBLK - 1) - blk
                self.ld(XST[0].rearrange("s t c x -> s (t c x)"), sc0[:, blk, :], ['SCANIN'], ['XST0'])
                self.ld(XST[1].rearrange("s t c x -> s (t c x)"), sc1[:, rb, :], ['SCANIN'], ['XST1'])
                for s in range(ns):
                    self.ld(X[s * 8:(s + 1) * 8, :, :, :].rearrange("h t c j -> h (t c) j"),
                            XST[0, s].rearrange("t c (h j) -> h (t c) j", h=8), ['XST0'], ['X'])
                    for c in range(6):
                        self.ld(X[ns * 8 + s * 8:ns * 8 + (s + 1) * 8, ::-1, c, :],
                                XST[1, s][:, c, :].rearrange("t (h j) -> h t j", h=8), ['XST1'], ['X'], q=('sp' if c % 2 else 'act'))
                for t in range(TB):
                    kap, rr, vv, ww, bet, kd = (X[:, t, c, :] for c in range(6))
                    bj = lambda a: a.unsqueeze(1).to_broadcast([NP, 64, 64])
                    bi = lambda a: a.unsqueeze(2).to_broadcast([NP, 64, 64])
                    self.tt(tmp[:], St[:], bj(kap), ALU.mult, ['St', 'X'], ['tmp'])
                    S.op('dve', lambda e: e.tensor_reduce(out=sa[:], in_=tmp[:], op=ALU.add, axis=AX.X), r=['tmp'], w=['sa'])
                    self.tt(St[:], St[:], bj(ww), ALU.mult, ['St', 'X'], ['St'])
                    self.tt(tmp[:], bi(sa[:]), bj(bet), ALU.mult, ['sa', 'X'], ['tmp'])
                    self.tt(St[:], St[:], tmp[:], ALU.subtract, ['St', 'tmp'], ['St'])
                    self.tt(tmp[:], bi(vv), bj(kd), ALU.mult, ['X'], ['tmp'])
                    self.tt(St[:], St[:], tmp[:], ALU.add, ['St', 'tmp'], ['St'])
                    self.tt(tmp[:], St[:], bj(rr), ALU.mult, ['St', 'X'], ['tmp'])
                    S.op('dve', lambda e, t=t: e.tensor_reduce(out=Y[:, t, :], in_=tmp[:], op=ALU.add, axis=AX.X), r=['tmp'], w=['Y'])
                for s in range(ns):
                    self.ld(YST[0, s].rearrange("t (h i) -> h t i", h=8), Y[s * 8:(s + 1) * 8, :, :], ['Y'], ['YST0'])
                    self.ld(YST[1, s].rearrange("t (h i) -> h t i", h=8), Y[ns * 8 + s * 8:ns * 8 + (s + 1) * 8, ::-1, :], ['Y'], ['YST1'])
                self.ld(ys0[:, blk, :], YST[0].rearrange("s t x -> s (t x)"), ['YST0'], ['YSCAN'])
                self.ld(ys1[:, rb, :], YST[1].rearrange("s t x -> s (t x)"), ['YST1'], ['YSCAN'])

    def phase_B3(self, li, YSCAN, BONUS, GATE, OT):
        S = self.S
        with self.scope():
            self.load_consts()
            lg_b = self.tile("lxg", [128, 512])
            lb_b = self.tile("lxb", [128, 512])
            self.ld(lg_b[:], self.ins['rwkv_lnx_g'][li].partition_broadcast(128), [], ['lx'])
            self.ld(lb_b[:], self.ins['rwkv_lnx_b'][li].partition_broadcast(128), [], ['lx'])
            yt = [[self.tile(f"y{i}{d}", [128, 512]) for d in range(2)] for i in range(2)]
            bt = [self.tile(f"bn{i}", [128, 512]) for i in range(2)]
            gt = [self.tile(f"gg{i}", [128, 512]) for i in range(2)]
            sq = self.tile("sq", [128, 512])
            sm = self.tile("sm", [128, 40])
            ob = self.tile("obT", [128, 4, 128], BF16)
            Yc, BONc, GATc = self.Yc, self.BONc, self.GATc
            eps = self.tile("epsg", [128, 1])
            self.ms(eps[:], 64e-5, ['epsg'])
            self.dq = 'sp'
            for s in S.iter(self.nseq, True):
                for d in range(2):
                    self.ld(Yc[d], YSCAN[d][s], ['YSCAN'], ['Yc'])
                self.ld(BONc, BONUS[s], ['BONUS'], ['BONc'])
                self.ld(GATc, GATE[s], ['GATE'], ['GATc'])
                for j in range(NT):
                    rows, tg = trows(j), toff(j)
                    R = slice(0, rows)
                    i = j % 2
                    y0, y1 = yt[i]
                    self.ld(y0[R, :], Yc[0, tg:tg + rows, :], ['Yc'], [f'y{i}0'])
                    self.ld(y1[R, :], Yc[1, tg:tg + rows, :], ['Yc'], [f'y{i}1'])
                    self.ld(bt[i][R, :], BONc[tg:tg + rows, :], ['BONc'], [f'bn{i}'])
                    self.ld(gt[i][R, :], GATc[tg:tg + rows, :], ['GATc'], [f'gg{i}'])
                    self.tt(y0[R, :], y0[R, :], y1[R, :], ALU.add, [f'y{i}0', f'y{i}1'], [f'y{i}0'])
                    v3 = lambda a: a.rearrange("p (h j) -> p h j", h=8)
                    S.op('dve', lambda e, R=R, y0=y0: e.tensor_reduce(out=sm[R, 0:8], in_=v3(y0[R, :]), op=ALU.add, axis=AX.X),
                         r=[f'y{i}0'], w=['sm'])
                    self.act(sq[R, :], y0[R, :], AF.Square, [f'y{i}0'], ['sq'])
                    S.op('dve', lambda e, R=R: e.tensor_reduce(out=sm[R, 8:16], in_=v3(sq[R, :]), op=ALU.add, axis=AX.X), r=['sq'], w=['sm'])
                    self.ts(sm[R, 0:16], sm[R, 0:16], 1.0 / 64.0, None, ALU.mult, None, ['sm'], ['sm'])
                    self.tt(sm[R, 16:24], sm[R, 0:8], sm[R, 0:8], ALU.mult, ['sm'], ['sm'])
                    self.tt(sm[R, 8:16], sm[R, 8:16], sm[R, 16:24], ALU.subtract, ['sm'], ['sm'])
                    self.act(sm[R, 8:16], sm[R, 8:16], AF.Sqrt, ['sm', 'epsg'], ['sm'], bias=eps[R, :])
                    S.op('dve', lambda e, R=R: e.reciprocal(out=sm[R, 24:32], in_=sm[R, 8:16]), r=['sm'], w=['sm'])
                    bh = lambda a: a.unsqueeze(2).to_broadcast([rows, 8, 64])
                    self.tt(v3(y0[R, :]), v3(y0[R, :]), bh(sm[R, 0:8]), ALU.subtract, [f'y{i}0', 'sm'], [f'y{i}0'])
                    self.tt(v3(y0[R, :]), v3(y0[R, :]), bh(sm[R, 24:32]), ALU.mult, [f'y{i}0', 'sm'], [f'y{i}0'])
                    self.tt(y0[R, :], y0[R, :], lg_b[R, :], ALU.mult, [f'y{i}0', 'lx'], [f'y{i}0'], eng='pool')
                    self.tt(y0[R, :], y0[R, :], lb_b[R, :], ALU.add, [f'y{i}0', 'lx'], [f'y{i}0'], eng='pool')
                    self.tt(y0[R, :], y0[R, :], bt[i][R, :], ALU.add, [f'y{i}0', f'bn{i}'], [f'y{i}0'])
                    self.tt(y0[R, :], y0[R, :], gt[i][R, :], ALU.mult, [f'y{i}0', f'gg{i}'], [f'y{i}0'])
                    bk = S.bank()

                    def fn(e, rows=rows, y0=y0, bk=bk):
                        ins = None
                        for q in range(4):
                            ins = e.transpose(out=self.ps[bk][:, q * 128:q * 128 + rows], in_=y0[0:rows, q * 128:(q + 1) * 128],
                                              identity=self.c['ident'][0:rows, 0:rows])
                        return ins
                    S.op('pe', fn, r=[f'y{i}0', 'ident'], w=[f'ps{bk}'])
                    self.cp(ob[:, :, 0:rows], self.ps[bk][:, :].rearrange("p (q c) -> p q c", q=4)[:, :, 0:rows], [f'ps{bk}'], ['ob'], eng='act')
                    self.ld(self.OTc.rearrange("(k p) t -> p k t", p=128)[:, :, tg:tg + rows], ob[:, :, 0:rows], ['ob'], ['OTc'])
                self.ld(OT[s][1], self.OTc, ['OTc'], ['OT'])

    Builder.phase_B1 = phase_B1
    Builder.phase_B2 = phase_B2
    Builder.phase_B3 = phase_B3


_rwkv_phases()


def _merge_phase():
    def phase_M(self, li, H, WIN, WBR, WOUT, OT, AFF):
        S = self.S
        with self.scope():
            self.load_consts()
            self.alloc_ln()
            hT = self.tile("hT", [128, 8, L], BF16)
            self.tl_hx = [self.tile(f"hx{i}", [128, D]) for i in range(2)]
            wbr = self.tile("wbr", [128, 16, D], BF16)
            wout = self.tile("wout", [128, 8, D], BF16)
            self.ld(wbr[:], WBR[li].rearrange("(k p) n -> p k n", p=128), [], ['wbr'])
            self.ld(wout[:], WOUT[li].rearrange("(k p) n -> p k n", p=128), [], ['wout'])
            r32 = self.tile("r32", [128, 8, 16])
            rtb = self.tile("rtb", [128, 8, 16], BF16)
            self.ld(r32[:], self.ins['router'][li].rearrange("(k p) e -> p k e", p=128), [], ['r32'])
            self.cp(rtb[:], r32[:], ['r32'], ['rtb'])
            g_b = self.tile("lng", [128, D])
            b_b = self.tile("lnb", [128, D])
            self.ld(g_b[:], self.ins['ln1_g'][li].partition_broadcast(128), [], ['lng'])
            self.ld(b_b[:], self.ins['ln1_b'][li].partition_broadcast(128), [], ['lnb'])
            otb = self.tile("otb", [128, 16, 512], BF16)
            wbm = [self.tile(f"wbm{i}", [128, 8, 512], BF16) for i in range(2)]
            acc = self.tile("acc", [128, 8, 512])
            mT = self.tile("mT", [128, 8, 512], BF16)
            gtt = [self.tile(f"gt{i}", [128, 512]) for i in range(2)]
            hres = [self.tile(f"hres{i}", [128, D]) for i in range(2)]
            xo = [self.tile(f"xo{i}", [128, D]) for i in range(2)]
            h1T = self.tile("h1T", [128, 8, 128], BF16)
            lgt = self.tile("lgt", [128, 16])
            sm = self.tile("smx", [128, 8])
            aft = [self.tile(f"aff{i}", [128, 16]) for i in range(2)]
            OT4c, AFFc = self.OT4c, self.AFFc
            nw = 0
            self.dq = 'act'
            for s in S.iter(self.nseq, False):
                self.build_hT(H, s, hT, self.c['ident'])
                self.ld(OT4c, OT[s], ['OT'], ['OT4c'], q=self.dq)
                for (t0, n) in TOKB:
                    self.ld(otb[:, :, 0:n], OT4c.rearrange("b (k p) t -> p (b k) t", p=128)[:, :, t0:t0 + n], ['OT4c'], ['otb'])
                    for nb in range(4):
                        for half in range(2):
                            wi = nw % 2
                            nw += 1
                            self.load_wblk(wbm[wi], f'wbm{wi}', WIN, li, COL['m'] + nb * 1024 + half * 512, 512)
                            for q in range(4):
                                dt = half * 4 + q
                                bp = S.bank()
                                self.mm(self.ps[bp][:, 0:n], [(wbr[:, nb * 4 + k, dt * 128:(dt + 1) * 128], otb[:, nb * 4 + k, 0:n]) for k in range(4)],
                                        ['wbr', 'otb'], [f'ps{bp}'])
                                bg = S.bank()
                                self.mm(self.ps[bg][:, 0:n], [(wbm[wi][:, k, q * 128:(q + 1) * 128], hT[:, k, t0:t0 + n]) for k in range(8)],
                                        [f'wbm{wi}', 'hT'], [f'ps{bg}'])
                                gi = (dt + nb) % 2
                                self.act(gtt[gi][:, 0:n], self.ps[bg][:, 0:n], AF.Sigmoid, [f'ps{bg}'], [f'gt{gi}'])
                                if nb == 0:
                                    self.tt(acc[:, dt, 0:n], gtt[gi][:, 0:n], self.ps[bp][:, 0:n], ALU.mult, [f'gt{gi}', f'ps{bp}'], [f'acc{dt}'])
                                else:
                                    self.tt(gtt[gi][:, 0:n], gtt[gi][:, 0:n], self.ps[bp][:, 0:n], ALU.mult, [f'gt{gi}', f'ps{bp}'], [f'gt{gi}'])
                                    self.tt(acc[:, dt, 0:n], acc[:, dt, 0:n], gtt[gi][:, 0:n], ALU.add, [f'acc{dt}', f'gt{gi}'], [f'acc{dt}'], eng='pool')
                    self.cp(mT[:, :, 0:n], acc[:, :, 0:n], [f'acc{d_}' for d_ in range(8)], ['mT'], eng='act')
                    jl = [j for j in range(NT) if t0 <= toff(j) < t0 + n]
                    for j in jl:
                        rows, lo, tg = trows(j), toff(j) - t0, toff(j)
                        R = slice(0, rows)
                        i = j % 2
                        self.ld(hres[i][R, :], self.Hcur[tg:tg + rows, :], ['Hcur'], [f'hres{i}'])
                        for half in range(2):
                            bo = S.bank()
                            self.mm(self.ps[bo][R, :], [(mT[:, k, lo:lo + rows], wout[:, k, half * 512:(half + 1) * 512]) for k in range(8)],
                                    ['mT', 'wout'], [f'ps{bo}'])
                            self.stt(xo[i][R, half * 512:(half + 1) * 512], hres[i][R, half * 512:(half + 1) * 512], ALPHA,
                                     self.ps[bo][R, :], ALU.mult, ALU.add, [f'hres{i}', f'ps{bo}'], [f'xo{i}'])
                        self.layer_norm_tile(xo[i][R, :], rows, g_b, b_b, hres[i][R, :], f'xo{i}', f'hres{i}')
                        self.ld(self.Hcur[tg:tg + rows, :], hres[i][R, :], [f'hres{i}'], ['Hcur'])
                        self.router_aff(hres[i], f'hres{i}', rows, rtb, h1T, lgt, sm, aft[i], f'aff{i}')
                        self.ld(AFFc[tg:tg + rows, :], aft[i][R, :], [f'aff{i}'], ['AFFc'])
                self.ld(H[s], self.Hcur, ['Hcur'], ['H'], q=self.dq)
                self.ld(AFF[s], AFFc, ['AFFc'], ['AFF'], q=self.dq)

    def router_aff(self, h1, hkey, rows, rtb, h1T, lgt, sm, aff, akey):
        S = self.S
        R = slice(0, rows)
        for g in range(2):
            b = S.bank()

            def fn(e, g=g, b=b):
                ins = None
                for q in range(4):
                    kc = g * 4 + q
                    ins = e.transpose(out=self.ps[b][:, q * 128:q * 128 + rows], in_=h1[0:rows, kc * 128:(kc + 1) * 128],
                                      identity=self.c['ident'][0:rows, 0:rows])
                return ins
            S.op('pe', fn, r=[hkey, 'ident'], w=[f'ps{b}'])
            self.cp(h1T[:, g * 4:(g + 1) * 4, 0:rows], self.ps[b][:, :].rearrange("p (q c) -> p q c", q=4)[:, :, 0:rows],
                    [f'ps{b}'], ['h1T'], eng=('act' if g == 0 else 'dve'))
        b = S.bank()
        self.mm(self.ps[b][R, 0:16], [(h1T[:, k, 0:rows], rtb[:, k, :]) for k in range(8)], ['h1T', 'rtb'], [f'ps{b}'])
        self.cp(lgt[R, :], self.ps[b][R, 0:16], [f'ps{b}'], ['lgt'])
        S.op('dve', lambda e: e.tensor_reduce(out=sm[R, 0:1], in_=lgt[R, :], op=ALU.max, axis=AX.X), r=['lgt'], w=['smx'])
        self.ts(sm[R, 1:2], sm[R, 0:1], -1.0, None, ALU.mult, None, ['smx'], ['smx'])
        self.act(lgt[R, :], lgt[R, :], AF.Exp, ['lgt', 'smx'], ['lgt'], bias=sm[R, 1:2])
        S.op('dve', lambda e: e.tensor_reduce(out=sm[R, 2:3], in_=lgt[R, :], op=ALU.add, axis=AX.X), r=['lgt'], w=['smx'])
        S.op('dve', lambda e: e.reciprocal(out=sm[R, 3:4], in_=sm[R, 2:3]), r=['smx'], w=['smx'])
        self.ts(aff[R, :], lgt[R, :], sm[R, 3:4], None, ALU.mult, None, ['lgt', 'smx'], [akey])

    Builder.phase_M = phase_M
    Builder.router_aff = router_aff


_merge_phase()


WEIGHT_SHAPES = dict(
    meta=[16, D], ln_emb_g=[D], ln_emb_b=[D], w_in=[2, D, NIN], w_branch=[2, 4, 512, D], w_out=[2, D, D], rwkv_mu=[2, 1920],
    rwkv_w0=[2, 2, 512], rwkv_w2=[2, 2, 64, 512], rwkv_a0=[2, 2, 512], rwkv_a2=[2, 2, 64, 512], rwkv_g2=[2, 128, 512],
    rwkv_kk=[2, 512], rwkv_ka=[2, 512], rwkv_rk=[2, 512], rwkv_lnx_g=[2, 512], rwkv_lnx_b=[2, 512],
    ln1_g=[2, D], ln1_b=[2, D], router=[2, D, 16], ln2_g=[2, D], ln2_b=[2, D],
    exp_w1=[2, 16, D, 2048], exp_w3=[2, 16, D, 2048], exp_w2=[2, 16, 2048, D],
    pp=[2, 128, NPP], lru_bd=[2, 16, 128, 128], w_sw=[2, D, 512])


def declare_weights(B, names):
    for nm in names:
        B.din(nm, WEIGHT_SHAPES[nm])
    for k, v in host_consts().items():
        B.din(k, v.shape)


MIX_W = ['w_in', 'w_branch', 'w_out', 'rwkv_mu', 'rwkv_w0', 'rwkv_w2', 'rwkv_a0', 'rwkv_a2', 'rwkv_g2', 'rwkv_kk', 'rwkv_ka',
         'rwkv_rk', 'rwkv_lnx_g', 'rwkv_lnx_b', 'ln1_g', 'ln1_b', 'router', 'pp', 'lru_bd', 'w_sw']


def mixer_layer(B, li, H, AFF):
    ns = B.nseq
    WIN = B.dscr(f'WIN{li}', [2, D, NIN], BF16)
    WMIX = B.dscr(f'WMIX{li}', [2, 2 * D, 1920], BF16)
    WBR = B.dscr(f'WBR{li}', [2, 2048, D], BF16)
    WOUT = B.dscr(f'WOUT{li}', [2, D, D], BF16)
    WSW = B.dscr(f'WSW{li}', [2, D, 512], BF16)
    OT = B.dscr(f'OT{li}', [ns, 4, 512, L], BF16)
    SCANIN = [B.dscr(f'SCANIN{li}_{d}', [ns, L, 6, 512]) for d in range(2)]
    YSCAN = B.dscr(f'YSCAN{li}', [2, ns, L, 512])
    BONUS = B.dscr(f'BONUS{li}', [ns, L, 512])
    GATE = B.dscr(f'GATE{li}', [ns, L, 512])
    B.phase_W(li, WIN, WMIX, WBR, WOUT, WSW)
    B.phase_B1(li, H, WMIX, SCANIN, BONUS, GATE)
    B.phase_B2(SCANIN, YSCAN)
    B.phase_B3(li, YSCAN, BONUS, GATE, OT)
    B.phase_A(li, H, WIN, OT)
    B.phase_C(li, H, WIN, OT)
    B.phase_D(li, H, WIN, WSW, OT)
    B.phase_M(li, H, WIN, WBR, WOUT, OT, AFF)


def build_launch_A(nseq):
    B = Builder(nseq, 0, 0, 'A')
    B.din('x', [nseq, 2048, D])
    declare_weights(B, ['meta', 'ln_emb_g', 'ln_emb_b'] + MIX_W)
    oH = B.dout('oH', [nseq, L, D])
    oAFF = B.dout('oAFF', [nseq, L, 16])
    H = B.dscr('H', [nseq, L, D])
    AFF = B.dscr('AFF', [nseq, L, 16])
    B.phase_E(H)
    mixer_layer(B, 0, H, AFF)
    B.ld(oH, H, ['H'], ['oH'])
    B.ld(oAFF, AFF, ['AFF'], ['oAFF'])
    B.S.join(['sp'])
    return B


def host_weight_inputs(inp):
    d = {k: np.ascontiguousarray(inp[k], dtype=np.float32) for k in WEIGHT_SHAPES if k in inp}
    d['pp'] = pack_pp(inp)
    d['lru_bd'] = make_lru_bd(inp)
    d['w_sw'] = make_w_sw(inp)
    d.update(host_consts())
    return d


def _moe_phases():
    NIT = 34

    def phase_T(self, groups, THRS):
        S = self.S
        with self.scope():
            onesF = self.tile("onesF", [128, 128])
            self.ms(onesF[:], 1.0, ['onesF'])
            for gi, (aff_all, N, seqs) in enumerate(groups):
                F = N // 128
                assert F * 128 == N
                cap = float(N // 8)
                afa = self.tile(f"afa{gi}", [128, F, 16])
                msk = self.tile(f"msk{gi}", [128, F, 16])
                self.ld(afa[:], aff_all.rearrange("(p f) e -> p f e", p=128), [], ['afa'])
                lo = self.tile(f"lo{gi}", [128, 16])
                hi = self.tile(f"hi{gi}", [128, 16])
                mid = self.tile(f"mid{gi}", [128, 16])
                cnt = self.tile(f"cnt{gi}", [128, 16])
                cc = self.tile(f"cc{gi}", [128, 16])
                dd = self.tile(f"dd{gi}", [128, 16])
                self.ms(lo[:], 0.0, ['lo'])
                self.ms(hi[:], 1.0, ['hi'])
                for it in range(NIT):
                    self.tt(mid[:], lo[:], hi[:], ALU.add, ['lo', 'hi'], ['mid'])
                    self.ts(mid[:], mid[:], 0.5, None, ALU.mult, None, ['mid'], ['mid'])
                    self.tt(msk[:], afa[:], mid[:].unsqueeze(1).to_broadcast([128, F, 16]), ALU.is_ge, ['afa', 'mid'], ['msk'])
                    S.op('dve', lambda e: e.tensor_reduce(out=cnt[:], in_=msk[:].rearrange("p f e -> p e f"), op=ALU.add, axis=AX.X),
                         r=['msk'], w=['cnt'])
                    b = S.bank()
                    self.mm(self.ps[b][:, 0:16], [(onesF[:, :], cnt[:, :])], ['onesF', 'cnt'], [f'ps{b}'])
                    self.ts(cc[:], self.ps[b][:, 0:16], cap - 0.5, None, ALU.is_ge, None, [f'ps{b}'], ['cc'])
                    self.tt(dd[:], mid[:], lo[:], ALU.subtract, ['mid', 'lo'], ['dd'])
                    self.tt(dd[:], dd[:], cc[:], ALU.mult, ['dd', 'cc'], ['dd'])
                    self.tt(lo[:], lo[:], dd[:], ALU.add, ['lo', 'dd'], ['lo'])
                    self.tt(dd[:], hi[:], mid[:], ALU.subtract, ['hi', 'mid'], ['dd'])
                    self.tt(dd[:], dd[:], cc[:], ALU.mult, ['dd', 'cc'], ['dd'])
                    self.tt(hi[:], mid[:], dd[:], ALU.add, ['mid', 'dd'], ['hi'])
                for s in seqs:
                    self.ld(THRS[s:s + 1, :], lo[0:1, :], ['lo'], ['THRS'])

    def phase_M1(self, H, AFFL, THRS, XBUF, IDXD, GATD):
        S = self.S
        with self.scope():
            self.load_consts()
            triU = self.tile("triU", [128, 128])
            onesF = self.tile("onesF", [128, 128])
            self.ld(triU[:], self.ins['c_triu'], [], ['triU'])
            self.ms(onesF[:], 1.0, ['onesF'])
            zt = self.tile("zt", [128, 4096], BF16)
            self.ms(zt[:], 0.0, ['zt'])
            XB = [self.dscr(f"XB{self.uid}_{e}", [CAPL + 128, D], BF16) for e in range(NEXP)]
            trash = self.tile("trash", [128, 16])
            self.ld(trash[:], self.ins['c_trash'], [], ['trash'])
            for e in range(NEXP):
                for q in range(CAPL // 512):
                    self.ld(XB[e][q * 512:(q + 1) * 512, :].rearrange("(p a) d -> p (a d)", p=128), zt[:, :], ['zt'], ['XB'])
            carry = self.tile("carry", [128, 16])
            self.ms(carry[:], 0.0, ['carry'])
            thr = self.tile("thr", [128, 16])
            af = [self.tile(f"af{i}", [128, 16]) for i in range(2)]
            sel = [self.tile(f"sel{i}", [128, 16]) for i in range(2)]
            pos = [self.tile(f"pos{i}", [128, 16]) for i in range(2)]
            idx = [self.tile(f"idx{i}", [128, 16], I32) for i in range(2)]
            gat = [self.tile(f"gat{i}", [128, 16]) for i in range(2)]
            gpos = [self.tile(f"gpos{i}", [128, 16]) for i in range(2)]
            gidx = [self.tile(f"gidx{i}", [128, 16], I32) for i in range(2)]
            hx = [self.tile(f"mhx{i}", [128, D]) for i in range(2)]
            hb = [self.tile(f"mhb{i}", [128, D], BF16) for i in range(2)]
            for s in range(self.nseq):
                self.ld(thr[:], THRS[s].partition_broadcast(128), ['THRS'], ['thr'])
                for j in range(NT):
                    rows, tg = trows(j), toff(j)
                    R = slice(0, rows)
                    i = (s * NT + j) % 2
                    if rows < 128:
                        self.ms(af[i][:], 0.0, [f'af{i}'])
                        self.ms(hx[i][:], 0.0, [f'mhx{i}'])
                    self.ld(af[i][R, :], AFFL[s, tg:tg + rows, :], [], [f'af{i}'])
                    self.ld(hx[i][R, :], H[s, tg:tg + rows, :], ['H'], [f'mhx{i}'])
                    self.tt(sel[i][:], af[i][:], thr[:], ALU.is_ge, [f'af{i}', 'thr'], [f'sel{i}'])
                    self.tt(gat[i][:], af[i][:], sel[i][:], ALU.mult, [f'af{i}', f'sel{i}'], [f'gat{i}'], eng='pool')
                    b = S.bank()
                    self.mm(self.ps[b][:, 0:16], [(triU[:, :], sel[i][:, :])], ['triU', f'sel{i}'], [f'ps{b}'])
                    b2 = S.bank()
                    self.mm(self.ps[b2][:, 0:16], [(onesF[:, :], sel[i][:, :])], ['onesF', f'sel{i}'], [f'ps{b2}'])
                    self.tt(pos[i][:], self.ps[b][:, 0:16], carry[:], ALU.add, [f'ps{b}', 'carry'], [f'pos{i}'])
                    self.tt(carry[:], carry[:], self.ps[b2][:, 0:16], ALU.add, ['carry', f'ps{b2}', f'pos{i}'], ['carry'])
                    self.tt(gpos[i][:], pos[i][:], sel[i][:], ALU.mult, [f'pos{i}', f'sel{i}'], [f'gpos{i}'], eng='pool')
                    self.cp(gidx[i][:], gpos[i][:], [f'gpos{i}'], [f'gidx{i}'], eng='pool')
                    self.ld(IDXD[s, j], gidx[i][:], [f'gidx{i}'], ['IDXD'])
                    self.tt(pos[i][:], pos[i][:], trash[:], ALU.subtract, [f'pos{i}', 'trash'], [f'pos{i}'])
                    self.tt(pos[i][:], pos[i][:], sel[i][:], ALU.mult, [f'pos{i}', f'sel{i}'], [f'pos{i}'])
                    self.tt(pos[i][:], pos[i][:], trash[:], ALU.add, [f'pos{i}', 'trash'], [f'pos{i}'])
                    self.cp(idx[i][:], pos[i][:], [f'pos{i}'], [f'idx{i}'])
                    self.ld(GATD[s, j], gat[i][:], [f'gat{i}'], ['GATD'])
                    self.cp(hb[i][:], hx[i][:], [f'mhx{i}'], [f'mhb{i}'], eng='act')
                    for e in range(NEXP):
                        S.dma(lambda g, e=e, i=i: g.indirect_dma_start(
                            out=XB[e], out_offset=bass.IndirectOffsetOnAxis(ap=idx[i][:, e:e + 1], axis=0),
                            in_=hb[i][:, :], in_offset=None, bounds_check=self.bcreg2, oob_is_err=False),
                            r=[f'idx{i}', f'mhb{i}'], w=['XB', f'scat{e % 4}'], q='pool')
            for e in range(NEXP):
                self.ld(XBUF[e], XB[e][0:CAPL, :], ['XB'], ['XBUF'])
            S.join(['pool'])
            self.nc.gpsimd.drain()

    def phase_M2(self, li, XBUF, YBUF):
        S = self.S
        w1, w3, w2 = self.ins['exp_w1'], self.ins['exp_w3'], self.ins['exp_w2']
        XBc, YBc = self.XBc, self.YBc
        with self.scope():
            self.load_consts()
            w1b = self.tile("w1b", [128, 8, 2048], BF16)
            w3b = self.tile("w3b", [128, 8, 2048], BF16)
            w2b = self.tile("w2b", [128, 16, D], BF16)
            stg = self.tile("stg", [128, 8192])
            xT = self.tile("xT", [128, 8, 512], BF16)
            hTe = self.tile("hTe", [128, 16, 512], BF16)
            xtok = [self.tile(f"xtok{i}", [128, D], BF16) for i in range(2)]
            yo = [self.tile(f"yo{i}", [128, D]) for i in range(2)]
            tm = [self.tile(f"tm{i}", [128, 512]) for i in range(2)]
            engs = ['pool', 'dve', 'act']
            self.dq = 'sp'
            for e in S.iter(NEXP, False):
                self.ld(XBc, XBUF[e], ['XBUF'], ['XBc'], q=self.dq)
                n = 0
                for (wsrc, wdst, key) in ((w1, w1b, 'w1b'), (w3, w3b, 'w3b')):
                    for hf in range(2):
                        self.ld(stg[:, :].rearrange("p (k n) -> p k n", k=4),
                                wsrc[li, e][hf * 512:(hf + 1) * 512, :].rearrange("(k p) n -> p k n", p=128), [], ['stg'], q=self.dq)
                        self.cp(wdst[:, hf * 4:(hf + 1) * 4, :], stg[:, :].rearrange("p (k n) -> p k n", k=4), ['stg'], [key], eng=engs[n % 3])
                        n += 1
                for hf in range(2):
                    self.ld(stg[:, :].rearrange("p (k n) -> p k n", k=8),
                            w2[li, e][hf * 1024:(hf + 1) * 1024, :].rearrange("(k p) n -> p k n", p=128), [], ['stg'], q=self.dq)
                    self.cp(w2b[:, hf * 8:(hf + 1) * 8, :], stg[:, :].rearrange("p (k n) -> p k n", k=8), ['stg'], ['w2b'], eng=engs[n % 3])
                    n += 1
                for sb in range(CAPL // 512):
                    for st in range(4):
                        i = st % 2
                        r0 = sb * 512 + st * 128
                        self.ld(xtok[i][:, :], XBc[r0:r0 + 128, :], ['XBc'], [f'xtok{i}'])
                        for g in range(2):
                            b = S.bank()
                            groups = [(self.ps[b][:, q * 128:(q + 1) * 128],
                                       [(xtok[i][:, (g * 4 + q) * 128:(g * 4 + q + 1) * 128], self.c['identb'][:, :])]) for q in range(4)]
                            self.mms(groups, [f'xtok{i}', 'identb'], [f'ps{b}'])
                            self.cp(xT[:, g * 4:(g + 1) * 4, st * 128:(st + 1) * 128], self.ps[b][:, :].rearrange("p (q c) -> p q c", q=4),
                                    [f'ps{b}'], ['xT'], eng=('act' if g == 0 else 'dve'))
                    for ft in range(16):
                        b1 = S.bank()
                        self.mm(self.ps[b1][:, :], [(w1b[:, k, ft * 128:(ft + 1) * 128], xT[:, k, :]) for k in range(8)], ['w1b', 'xT'], [f'ps{b1}'])
                        b3 = S.bank()
                        self.mm(self.ps[b3][:, :], [(w3b[:, k, ft * 128:(ft + 1) * 128], xT[:, k, :]) for k in range(8)], ['w3b', 'xT'], [f'ps{b3}'])
                        ti = ft % 2
                        self.act(tm[ti][:, :], self.ps[b1][:, :], AF.Silu, [f'ps{b1}'], [f'tm{ti}'])
                        self.tt(hTe[:, ft, :], tm[ti][:, :], self.ps[b3][:, :], ALU.mult, [f'tm{ti}', f'ps{b3}'], ['hTe'])
                    for st in range(4):
                        i = st % 2
                        r0 = sb * 512 + st * 128
                        for hf in range(2):
                            b = S.bank()
                            self.mm(self.ps[b][:, :], [(hTe[:, ft, st * 128:(st + 1) * 128], w2b[:, ft, hf * 512:(hf + 1) * 512]) for ft in range(16)],
                                    ['hTe', 'w2b'], [f'ps{b}'])
                            self.cp(yo[i][:, hf * 512:(hf + 1) * 512], self.ps[b][:, :], [f'ps{b}'], [f'yo{i}'], eng=('act' if hf == 0 else 'dve'))
                        self.ld(YBc[r0:r0 + 128, :], yo[i][:, :], [f'yo{i}'], ['YBc'])
                self.ld(YBUF[e], YBc, ['YBc'], ['YBUF'], q=self.dq)

    def phase_M3(self, li, H, YBUF, IDXD, GATD, final, OUT):
        S = self.S
        with self.scope():
            self.load_consts()
            self.alloc_ln()
            g_b = self.tile("lng", [128, D])
            b_b = self.tile("lnb", [128, D])
            self.ld(g_b[:], self.ins['ln2_g'][li].partition_broadcast(128), [], ['lng'])
            self.ld(b_b[:], self.ins['ln2_b'][li].partition_broadcast(128), [], ['lnb'])
            G = [self.tile(f"G{i}", [128, D]) for i in range(4)]
            for i in range(4):
                self.ms(G[i][:], 0.0, [f'G{i}'])
            acc = [self.tile(f"macc{i}", [128, D]) for i in range(2)]
            hx = [self.tile(f"m3h{i}", [128, D]) for i in range(2)]
            idx = [self.tile(f"idx{i}", [128, 16], I32) for i in range(2)]
            gat = [self.tile(f"gat{i}", [128, 16]) for i in range(2)]
            ng = 0
            YB = [self.dscr(f"YB{self.uid}_{e}", [CAPL, D]) for e in range(NEXP)]
            for e in range(NEXP):
                self.ld(YB[e], YBUF[e], ['YBUF'], ['YB'])
            for s in range(self.nseq):
                for j in range(NT):
                    rows, tg = trows(j), toff(j)
                    R = slice(0, rows)
                    i = (s * NT + j) % 2
                    self.ld(idx[i][:], IDXD[s, j], ['IDXD'], [f'idx{i}'])
                    self.ld(gat[i][:], GATD[s, j], ['GATD'], [f'gat{i}'])
                    self.ld(hx[i][R, :], H[s, tg:tg + rows, :], ['H'], [f'm3h{i}'])
                    for e in range(NEXP):
                        gi = ng % 4
                        ng += 1
                        S.dma(lambda g, e=e, i=i, gi=gi: g.indirect_dma_start(
                            out=G[gi][:, :], out_offset=None, in_=YB[e],
                            in_offset=bass.IndirectOffsetOnAxis(ap=idx[i][:, e:e + 1], axis=0), bounds_check=self.bcreg, oob_is_err=False),
                            r=[f'idx{i}', 'YB'], w=[f'G{gi}'], q='pool')
                        if e == 0:
                            self.ts(acc[i][:], G[gi][:], gat[i][:, 0:1], None, ALU.mult, None, [f'G{gi}', f'gat{i}'], [f'macc{i}'])
                        else:
                            self.stt(acc[i][:], G[gi][:], gat[i][:, e:e + 1], acc[i][:], ALU.mult, ALU.add, [f'G{gi}', f'gat{i}', f'macc{i}'], [f'macc{i}'])
                    self.stt(acc[i][R, :], hx[i][R, :], ALPHA, acc[i][R, :], ALU.mult, ALU.add, [f'm3h{i}', f'macc{i}'], [f'macc{i}'])
                    self.layer_norm_tile(acc[i][R, :], rows, g_b, b_b, hx[i][R, :], f'macc{i}', f'm3h{i}')
                    if not final:
                        self.ld(H[s, tg:tg + rows, :], hx[i][R, :], [f'm3h{i}'], ['H'])
                    elif j > 0:
                        self.ld(OUT[s, tg - NMETA:tg - NMETA + rows, :], hx[i][R, :], [f'm3h{i}'], ['OUT'])

    Builder.phase_T = phase_T
    Builder.phase_M1 = phase_M1
    Builder.phase_M2 = phase_M2
    Builder.phase_M3 = phase_M3


_moe_phases()


def moe_layer(B, li, H, AFFL, aff_groups, final, OUT):
    ns = B.nseq
    THRS = B.dscr(f'THRS{li}', [ns, 16])
    XBUF = B.dscr(f'XBUF{li}', [NEXP, CAPL, D], BF16)
    YBUF = B.dscr(f'YBUF{li}', [NEXP, CAPL, D])
    IDXD = B.dscr(f'IDXD{li}', [ns, NT, 128, 16], I32)
    GATD = B.dscr(f'GATD{li}', [ns, NT, 128, 16])
    B.phase_T(aff_groups, THRS)
    B.phase_M1(H, AFFL, THRS, XBUF, IDXD, GATD)
    B.phase_M2(li, XBUF, YBUF)
    B.phase_M3(li, H, YBUF, IDXD, GATD, final, OUT)


def build_launch_BC(nseq, nsp, ncore, last):
    li = 1 if last else 0
    B = Builder(nseq, nsp, ncore, 'C' if last else 'B')
    names = ['router', 'ln2_g', 'ln2_b', 'exp_w1', 'exp_w3', 'exp_w2']
    if not last:
        names = sorted(set(names + MIX_W))
    declare_weights(B, names)
    hin = B.din('hin', [nseq, L, D])
    affl = B.din('aff_loc', [nseq, L, 16])
    nss = nseq - nsp
    NP_, NS_ = ncore * nsp * L, ncore * nss * L
    groups = []
    if nsp:
        groups.append((B.din('aff_p', [NP_, 16]), NP_, list(range(nsp))))
    if nss:
        groups.append((B.din('aff_s', [NS_, 16]), NS_, list(range(nsp, nseq))))
    H = B.dscr('H', [nseq, L, D])
    B.ld(H, hin, [], ['H'])
    if last:
        OUT = B.dout('y', [nseq, 2048, D])
        moe_layer(B, li, H, affl, groups, True, OUT)
    else:
        moe_layer(B, li, H, affl, groups, False, None)
        AFF = B.dscr('AFF', [nseq, L, 16])
        mixer_layer(B, 1, H, AFF)
        oH = B.dout('oH', [nseq, L, D])
        oAFF = B.dout('oAFF', [nseq, L, 16])
        B.ld(oH, H, ['H'], ['oH'])
        B.ld(oAFF, AFF, ['AFF'], ['oAFF'])
    B.S.join(['sp', 'pool', 'act'])
    return B


def _moe2_phases():
    WS = 40

    def phase_N1(self, H, AFFL, THRS, XT, PGTD):
        S = self.S
        with self.scope():
            self.load_consts()
            triU = self.tile("triU", [128, 128])
            self.ld(triU[:], self.ins['c_triu'], [], ['triU'])
            iw = self.tile("iotaw", [128, WS])
            self.ld(iw[:], self.ins['c_iotaw'], [], ['iotaw'])
            thr = self.tile("thr", [128, 16])
            af = [self.tile(f"af{i}", [128, 16]) for i in range(2)]
            sel = [self.tile(f"sel{i}", [128, 16]) for i in range(2)]
            lps = [self.tile(f"lps{i}", [128, 16]) for i in range(2)]
            gat = [self.tile(f"gat{i}", [128, 16]) for i in range(2)]
            hx = [self.tile(f"mhx{i}", [128, D]) for i in range(2)]
            hb = [self.tile(f"mhb{i}", [128, D], BF16) for i in range(2)]
            Pm = [self.tile(f"Pm{i}", [128, 16, WS], BF16) for i in range(2)]
            PG = [self.tile(f"PG{i}", [128, 16, WS], BF16) for i in range(2)]
            xw = [self.tile(f"xw{i}", [128, 8, WS], BF16) for i in range(4)]
            pgt = [self.tile(f"pgt{i}", [WS, 16, 128], BF16) for i in range(2)]
            nx = 0
            for s in range(self.nseq):
                self.ld(thr[:], THRS[s].partition_broadcast(128), ['THRS'], ['thr'])
                for j in range(NT):
                    rows, tg = trows(j), toff(j)
                    R = slice(0, rows)
                    tix = s * NT + j
                    i = tix % 2
                    if rows < 128:
                        self.ms(af[i][:], 0.0, [f'af{i}'])
                        self.ms(hx[i][:], 0.0, [f'mhx{i}'])
                    self.ld(af[i][R, :], AFFL[s, tg:tg + rows, :], [], [f'af{i}'])
                    self.ld(hx[i][R, :], H[s, tg:tg + rows, :], ['H'], [f'mhx{i}'])
                    self.tt(sel[i][:], af[i][:], thr[:], ALU.is_ge, [f'af{i}', 'thr'], [f'sel{i}'])
                    self.tt(gat[i][:], af[i][:], sel[i][:], ALU.mult, [f'af{i}', f'sel{i}'], [f'gat{i}'], eng='pool')
                    b = S.bank()
                    self.mm(self.ps[b][:, 0:16], [(triU[:, :], sel[i][:, :])], ['triU', f'sel{i}'], [f'ps{b}'])
                    self.cp(lps[i][:], self.ps[b][:, 0:16], [f'ps{b}'], [f'lps{i}'])
                    self.cp(hb[i][:], hx[i][:], [f'mhx{i}'], [f'mhb{i}'], eng='act')
                    for e in range(NEXP):
                        self.ts(Pm[i][:, e, :], iw[:, :], lps[i][:, e:e + 1], sel[i][:, e:e + 1], ALU.is_equal, ALU.mult,
                                ['iotaw', f'lps{i}', f'sel{i}'], [f'Pm{i}'], eng=('dve' if e % 2 == 0 else 'pool'))
                        self.ts(PG[i][:, e, :], iw[:, :], lps[i][:, e:e + 1], gat[i][:, e:e + 1], ALU.is_equal, ALU.mult,
                                ['iotaw', f'lps{i}', f'gat{i}'], [f'PG{i}'], eng=('pool' if e % 2 == 0 else 'dve'))
                    for e in range(NEXP):
                        b = S.bank()
                        groups = [(self.ps[b][:, k * WS:(k + 1) * WS], [(hb[i][:, k * 128:(k + 1) * 128], Pm[i][:, e, :])]) for k in range(8)]
                        self.mms(groups, [f'mhb{i}', f'Pm{i}'], [f'ps{b}'])
                        xi = nx % 4
                        nx += 1
                        self.cp(xw[xi][:], self.ps[b][:, 0:8 * WS].rearrange("p (k w) -> p k w", k=8), [f'ps{b}'], [f'xw{xi}'],
                                eng=('act' if e % 2 == 0 else 'dve'))
                        self.ld(XT[e].rearrange("(k p) n -> p k n", p=128)[:, :, tix * WS:(tix + 1) * WS], xw[xi][:], [f'xw{xi}'], ['XT'],
                                q=('sp' if e % 2 == 0 else 'act'))
                    for g in range(4):
                        b = S.bank()
                        groups = [(self.ps[b][0:WS, q * 128:(q + 1) * 128], [(PG[i][:, g * 4 + q, :], self.c['identb'][:, :])]) for q in range(4)]
                        self.mms(groups, [f'PG{i}', 'identb'], [f'ps{b}'])
                        self.cp(pgt[i][:, g * 4:(g + 1) * 4, :], self.ps[b][0:WS, :].rearrange("p (q t) -> p q t", q=4), [f'ps{b}'], [f'pgt{i}'],
                                eng=('act' if g % 2 == 0 else 'dve'))
                    self.ld(PGTD[s, j], pgt[i][:], [f'pgt{i}'], ['PGTD'])

    def phase_N2(self, li, XT, YB2, nslot):
        S = self.S
        w1, w3, w2 = self.ins['exp_w1'], self.ins['exp_w3'], self.ins['exp_w2']
        XTc = self.dscr(f'XTc{li}', [D, nslot], BF16)
        YTc = self.dscr(f'YTc{li}', [nslot, D], BF16)
        with self.scope():
            self.load_consts()
            w1b = self.tile("w1b", [128, 8, 2048], BF16)
            w3b = self.tile("w3b", [128, 8, 2048], BF16)
            w2b = self.tile("w2b", [128, 16, D], BF16)
            stg = self.tile("stg", [128, 8192])
            xT = self.tile("xT", [128, 8, 512], BF16)
            hTe = self.tile("hTe", [128, 16, 512], BF16)
            yo = [self.tile(f"yo{i}", [128, D], BF16) for i in range(2)]
            tm = [self.tile(f"tm{i}", [128, 512]) for i in range(2)]
            engs = ['pool', 'dve', 'act']
            self.dq = 'sp'
            for e in S.iter(NEXP, False):
                self.ld(XTc, XT[e], ['XT'], ['XTc'], q=self.dq)
                n = 0
                for (wsrc, wdst, key) in ((w1, w1b, 'w1b'), (w3, w3b, 'w3b')):
                    for hf in range(2):
                        self.ld(stg[:, :].rearrange("p (k n) -> p k n", k=4),
                                wsrc[li, e][hf * 512:(hf + 1) * 512, :].rearrange("(k p) n -> p k n", p=128), [], ['stg'], q=self.dq)
                        self.cp(wdst[:, hf * 4:(hf + 1) * 4, :], stg[:, :].rearrange("p (k n) -> p k n", k=4), ['stg'], [key], eng=engs[n % 3])
                        n += 1
                for hf in range(2):
                    self.ld(stg[:, :].rearrange("p (k n) -> p k n", k=8),
                            w2[li, e][hf * 1024:(hf + 1) * 1024, :].rearrange("(k p) n -> p k n", p=128), [], ['stg'], q=self.dq)
                    self.cp(w2b[:, hf * 8:(hf + 1) * 8, :], stg[:, :].rearrange("p (k n) -> p k n", k=8), ['stg'], ['w2b'], eng=engs[n % 3])
                    n += 1
                for s0 in range(0, nslot, 512):
                    nb = min(512, nslot - s0)
                    self.ld(xT[:, :, 0:nb], XTc.rearrange("(k p) n -> p k n", p=128)[:, :, s0:s0 + nb], ['XTc'], ['xT'])
                    for ft in range(16):
                        b1 = S.bank()
                        self.mm(self.ps[b1][:, 0:nb], [(w1b[:, k, ft * 128:(ft + 1) * 128], xT[:, k, 0:nb]) for k in range(8)], ['w1b', 'xT'], [f'ps{b1}'])
                        b3 = S.bank()
                        self.mm(self.ps[b3][:, 0:nb], [(w3b[:, k, ft * 128:(ft + 1) * 128], xT[:, k, 0:nb]) for k in range(8)], ['w3b', 'xT'], [f'ps{b3}'])
                        ti = ft % 2
                        self.act(tm[ti][:, 0:nb], self.ps[b1][:, 0:nb], AF.Silu, [f'ps{b1}'], [f'tm{ti}'])
                        self.tt(hTe[:, ft, 0:nb], tm[ti][:, 0:nb], self.ps[b3][:, 0:nb], ALU.mult, [f'tm{ti}', f'ps{b3}'], ['hTe'])
                    for c0 in range(0, nb, 128):
                        cr = min(128, nb - c0)
                        i = (c0 // 128) % 2
                        for hf in range(2):
                            b = S.bank()
                            self.mm(self.ps[b][0:cr, :], [(hTe[:, ft, c0:c0 + cr], w2b[:, ft, hf * 512:(hf + 1) * 512]) for ft in range(16)],
                                    ['hTe', 'w2b'], [f'ps{b}'])
                            self.cp(yo[i][0:cr, hf * 512:(hf + 1) * 512], self.ps[b][0:cr, :], [f'ps{b}'], [f'yo{i}'], eng=('act' if hf == 0 else 'dve'))
                        self.ld(YTc[s0 + c0:s0 + c0 + cr, :], yo[i][0:cr, :], [f'yo{i}'], ['YTc'])
                self.ld(YB2[e], YTc, ['YTc'], ['YB2'], q=self.dq)

    def phase_N3(self, li, H, YB2, PGTD, final, OUT):
        S = self.S
        with self.scope():
            self.load_consts()
            self.alloc_ln()
            g_b = self.tile("lng", [128, D])
            b_b = self.tile("lnb", [128, D])
            self.ld(g_b[:], self.ins['ln2_g'][li].partition_broadcast(128), [], ['lng'])
            self.ld(b_b[:], self.ins['ln2_b'][li].partition_broadcast(128), [], ['lnb'])
            pgt = [self.tile(f"pgt{i}", [WS, 16, 128], BF16) for i in range(2)]
            yw = [self.tile(f"yw{i}", [WS, 16, D], BF16) for i in range(2)]
            acc = [self.tile(f"macc{i}", [128, D]) for i in range(2)]
            hx = [self.tile(f"m3h{i}", [128, D]) for i in range(2)]
            for s in range(self.nseq):
                for j in range(NT):
                    rows, tg = trows(j), toff(j)
                    R = slice(0, rows)
                    tix = s * NT + j
                    i = tix % 2
                    self.ld(pgt[i][:], PGTD[s, j], ['PGTD'], [f'pgt{i}'])
                    self.ld(yw[i][:], YB2[:, tix * WS:(tix + 1) * WS, :].rearrange("e w d -> w e d"), ['YB2'], [f'yw{i}'], q='act')
                    self.ld(hx[i][R, :], H[s, tg:tg + rows, :], ['H'], [f'm3h{i}'])
                    for hf in range(2):
                        b = S.bank()
                        self.mm(self.ps[b][:, :], [(pgt[i][:, e, :], yw[i][:, e, hf * 512:(hf + 1) * 512]) for e in range(NEXP)],
                                [f'pgt{i}', f'yw{i}'], [f'ps{b}'])
                        self.stt(acc[i][R, hf * 512:(hf + 1) * 512], hx[i][R, hf * 512:(hf + 1) * 512], ALPHA, self.ps[b][R, :], ALU.mult, ALU.add,
                                 [f'm3h{i}', f'ps{b}'], [f'macc{i}'])
                    self.layer_norm_tile(acc[i][R, :], rows, g_b, b_b, hx[i][R, :], f'macc{i}', f'm3h{i}')
                    if not final:
                        self.ld(H[s, tg:tg + rows, :], hx[i][R, :], [f'm3h{i}'], ['H'])
                    elif j > 0:
                        self.ld(OUT[s, tg - NMETA:tg - NMETA + rows, :], hx[i][R, :], [f'm3h{i}'], ['OUT'])

    Builder.phase_N1 = phase_N1
    Builder.phase_N2 = phase_N2
    Builder.phase_N3 = phase_N3
    Builder.WS = WS


_moe2_phases()


def moe_layer(B, li, H, AFFL, aff_groups, final, OUT):
    ns = B.nseq
    nslot = ns * NT * B.WS
    THRS = B.dscr(f'THRS{li}', [ns, 16])
    XT = B.dscr(f'XT{li}', [NEXP, D, nslot], BF16)
    YB2 = B.dscr(f'YB2{li}', [NEXP, nslot, D], BF16)
    PGTD = B.dscr(f'PGTD{li}', [ns, NT, B.WS, 16, 128], BF16)
    B.phase_T(aff_groups, THRS)
    B.phase_N1(H, AFFL, THRS, XT, PGTD)
    B.phase_N2(li, XT, YB2, nslot)
    B.phase_N3(li, H, YB2, PGTD, final, OUT)


class _LayerView:
    def __init__(self, ap):
        self.ap = ap

    def __getitem__(self, idx):
        if isinstance(idx, tuple):
            return self.ap[(0,) + tuple(idx[1:])]
        return self.ap[0]


LAYERED = [k for k, v in WEIGHT_SHAPES.items() if v[0] == 2 and k not in ()]


def declare_weights(B, names):
    for nm in names:
        shp = list(WEIGHT_SHAPES[nm])
        if nm in LAYERED:
            shp[0] = 1
            ap = B.din(nm, shp)
            B.ins[nm] = _LayerView(ap)
        else:
            B.din(nm, shp)
    for k, v in host_consts().items():
        B.din(k, v.shape)


def _weights_for(hw, names, li):
    d = {}
    for nm in names:
        a = hw[nm]
        d[nm] = np.ascontiguousarray(a[li:li + 1]) if nm in LAYERED else a
    d.update(host_consts())
    return d


def _run(B, in_maps):
    for m in in_maps:
        for k in list(m.keys()):
            if k not in B.ins:
                del m[k]
    res = run_bass_kernel_spmd(B.nc, in_maps, core_ids=list(range(len(in_maps))))
    return res.results


def kernel(**inputs):
    inp = {k: np.asarray(v) for k, v in inputs.items()}
    ncore = 8
    xp, xs = inp['x_prompt'], inp['x_sample']
    nsp, nss = xp.shape[0] // ncore, xs.shape[0] // ncore
    nseq = nsp + nss
    hw = host_weight_inputs(inp)
    for k in ('meta', 'ln_emb_g', 'ln_emb_b'):
        hw[k] = np.ascontiguousarray(inp[k], dtype=np.float32)
    BA = build_launch_A(nseq)
    wa = _weights_for(hw, ['meta', 'ln_emb_g', 'ln_emb_b'] + MIX_W, 0)
    maps = []
    for c in range(ncore):
        m = dict(wa)
        m['x'] = np.ascontiguousarray(np.concatenate([xp[c * nsp:(c + 1) * nsp], xs[c * nss:(c + 1) * nss]], axis=0), dtype=np.float32)
        maps.append(m)
    ra = _run(BA, maps)
    Hc = [r['oH'] for r in ra]
    Ac = [r['oAFF'] for r in ra]

    def aff_tables(Ac):
        ap = np.ascontiguousarray(np.concatenate([a[0:nsp].reshape(-1, 16) for a in Ac], axis=0))
        as_ = np.ascontiguousarray(np.concatenate([a[nsp:nseq].reshape(-1, 16) for a in Ac], axis=0))
        return ap, as_
    BB = build_launch_BC(nseq, nsp, ncore, False)
    moe_w = ['router', 'ln2_g', 'ln2_b', 'exp_w1', 'exp_w3', 'exp_w2']
    wb_moe = _weights_for(hw, moe_w, 0)
    wb_mix = _weights_for(hw, MIX_W, 1)
    ap, as_ = aff_tables(Ac)
    maps = []
    for c in range(ncore):
        m = dict(wb_mix)
        m.update({k: wb_moe[k] for k in ('ln2_g', 'ln2_b', 'exp_w1', 'exp_w3', 'exp_w2')})
        m.update(hin=Hc[c], aff_loc=Ac[c], aff_p=ap, aff_s=as_)
        maps.append(m)
    rb = _run(BB, maps)
    Hc = [r['oH'] for r in rb]
    Ac = [r['oAFF'] for r in rb]
    BC = build_launch_BC(nseq, nsp, ncore, True)
    wc = _weights_for(hw, moe_w, 1)
    ap, as_ = aff_tables(Ac)
    maps = []
    for c in range(ncore):
        m = dict(wc)
        m.update(hin=Hc[c], aff_loc=Ac[c], aff_p=ap, aff_s=as_)
        maps.append(m)
    rc = _run(BC, maps)
    yp = np.concatenate([r['y'][0:nsp] for r in rc], axis=0).astype(np.float32)
    ys = np.concatenate([r['y'][nsp:nseq] for r in rc], axis=0).astype(np.float32)
    return (yp, ys)
```
